# Optimizing a Trainium2 kernel written in Bass

```python
import jax, jax.numpy as jnp
from jax import lax
import numpy as np

D_MODEL = 1024
BATCH = 8
SEQ = 4096
DEPTH = 1

HEAD_DIM = 128
HEADS_PER_GROUP = 4
DILATED_GROUPS = ((128, 1), (512, 4), (2048, 16))
N_GROUPS = 3
N_ATTN_HEADS = N_GROUPS * HEADS_PER_GROUP
ATTN_WIDTH = N_ATTN_HEADS * HEAD_DIM
ATTN_OUT_WIDTH = HEADS_PER_GROUP * HEAD_DIM
BLOCK = 128
CONV_WIDTH = D_MODEL
CONV_K = 3
D_FF = 4 * D_MODEL
N_MOD = 6
IN_COLS = 3 * ATTN_WIDTH + 3 * CONV_WIDTH + 2 * D_MODEL
EPS = 1e-6
NEG_INF = -1e30

kernel_name = "hybrid_dilated_attn_shortconv_gated_block"


def rmsnorm(x, g):
    xf = x.astype(jnp.float32)
    y = xf * lax.rsqrt(jnp.mean(xf * xf, axis=-1, keepdims=True) + EPS)
    return (y * g.astype(jnp.float32)).astype(x.dtype)


def alibi_slopes(n):
    return 2.0 ** (-8.0 * jnp.arange(1, n + 1, dtype=jnp.float32) / n)


def dilated_window_attention(q, k, v, window, dilation, slopes):
    B, S, H, E = q.shape
    n_win = window // dilation
    span = dilation * BLOCK
    s_pad = -(-S // span) * span
    L = s_pad // dilation
    nb = L // BLOCK

    def to_blocks(t):
        t = jnp.pad(t, ((0, 0), (0, s_pad - S), (0, 0), (0, 0)))
        t = t.reshape(B, L, dilation, H, E).transpose(0, 2, 1, 3, 4)
        return t.reshape(B, dilation, nb, BLOCK, H, E)

    def with_prev(t):
        prev = jnp.concatenate([jnp.zeros_like(t[:, :, :1]), t[:, :, :-1]], axis=2)
        return jnp.concatenate([prev, t], axis=3)

    qb = to_blocks(q)
    kw = with_prev(to_blocks(k))
    vw = with_prev(to_blocks(v))

    scores = jnp.einsum('brnqhe,brnkhe->brnhqk', qb, kw,
                        preferred_element_type=jnp.float32) * (E ** -0.5)
    qi = jnp.arange(BLOCK)[:, None]
    kj = jnp.arange(2 * BLOCK)[None, :]
    delta = BLOCK + qi - kj
    in_window = (delta >= 0) & (delta <= n_win)
    has_key = (jnp.arange(nb)[:, None, None] > 0) | (kj[None] >= BLOCK)
    mask = in_window[None] & has_key
    bias = -slopes[:, None, None] * (delta * dilation).astype(jnp.float32)
    scores = jnp.where(mask[:, None], scores + bias, NEG_INF)

    m = jnp.max(scores, axis=-1, keepdims=True)
    p = jnp.exp(scores - m)
    denom = jnp.sum(p, axis=-1, keepdims=True)
    o = jnp.einsum('brnhqk,brnkhe->brnqhe', p, vw.astype(jnp.float32))
    o = o / jnp.swapaxes(denom, 3, 4)
    lse = (m + jnp.log(denom))[..., 0]

    o = o.reshape(B, dilation, L, H, E).transpose(0, 2, 1, 3, 4).reshape(B, s_pad, H, E)[:, :S]
    lse = lse.transpose(0, 1, 2, 4, 3).reshape(B, dilation, L, H)
    lse = lse.transpose(0, 2, 1, 3).reshape(B, s_pad, H)[:, :S]
    return o, lse


def causal_short_conv(u, w):
    return lax.conv_general_dilated(
        u, w[:, None, :].astype(u.dtype), window_strides=(1,), padding=[(CONV_K - 1, 0)],
        dimension_numbers=('NWC', 'WIO', 'NWC'), feature_group_count=u.shape[-1])


def setup_inputs(seed: int = 0) -> dict:
    key = jax.random.key(seed)
    ks = jax.random.split(key, 16)
    f32 = jnp.float32
    nrm = lambda k, shape, s: jax.random.normal(k, shape, f32) * s
    return {
        "x": jax.random.normal(ks[0], (BATCH, SEQ, D_MODEL), f32),
        "c": jax.random.normal(ks[1], (BATCH, D_MODEL), f32),
        "w_ada": nrm(ks[2], (DEPTH, D_MODEL, N_MOD * D_MODEL), D_MODEL ** -0.5),
        "b_ada": nrm(ks[3], (DEPTH, N_MOD * D_MODEL), 0.01),
        "g_norm_mix": 1.0 + nrm(ks[4], (DEPTH, D_MODEL), 0.02),
        "w_in": nrm(ks[5], (DEPTH, D_MODEL, IN_COLS), D_MODEL ** -0.5),
        "b_gate": nrm(ks[6], (DEPTH, 2 * D_MODEL), 0.01),
        "conv_w": nrm(ks[7], (DEPTH, CONV_K, CONV_WIDTH), CONV_K ** -0.5),
        "w_branch_attn": nrm(ks[8], (DEPTH, ATTN_OUT_WIDTH, D_MODEL), ATTN_OUT_WIDTH ** -0.5),
        "w_branch_conv": nrm(ks[9], (DEPTH, CONV_WIDTH, D_MODEL), CONV_WIDTH ** -0.5),
        "w_out": nrm(ks[10], (DEPTH, D_MODEL, D_MODEL), D_MODEL ** -0.5),
        "g_norm_mlp": 1.0 + nrm(ks[11], (DEPTH, D_MODEL), 0.02),
        "w_mlp_in": nrm(ks[12], (DEPTH, D_MODEL, D_FF), D_MODEL ** -0.5),
        "w_mlp_out": nrm(ks[13], (DEPTH, D_FF, D_MODEL), D_FF ** -0.5),
        "g_norm_final": 1.0 + nrm(ks[14], (D_MODEL,), 0.02),
    }


def reference(x, c, w_ada, b_ada, g_norm_mix, w_in, b_gate, conv_w, w_branch_attn,
              w_branch_conv, w_out, g_norm_mlp, w_mlp_in, w_mlp_out, g_norm_final):
    B, S, D = x.shape
    slopes = alibi_slopes(N_ATTN_HEADS)
    widths = [ATTN_WIDTH] * 3 + [CONV_WIDTH] * 3 + [D_MODEL]
    split_pts = [int(s) for s in np.cumsum(widths)]
    c_act = jax.nn.silu(c)
    for l in range(DEPTH):
        mod = (c_act @ w_ada[l] + b_ada[l])[:, None, :]
        shift1, scale1, gate1, shift2, scale2, gate2 = jnp.split(mod, N_MOD, axis=-1)

        h = rmsnorm(x, g_norm_mix[l]) * (1.0 + scale1) + shift1
        proj = h @ w_in[l]
        q, k, v, cb, cc, cx, g_a, g_b = jnp.split(proj, split_pts, axis=-1)
        q = q.reshape(B, S, N_ATTN_HEADS, HEAD_DIM)
        k = k.reshape(B, S, N_ATTN_HEADS, HEAD_DIM)
        v = v.reshape(B, S, N_ATTN_HEADS, HEAD_DIM)

        outs, lses = [], []
        for gi, (window, dilation) in enumerate(DILATED_GROUPS):
            hs = slice(gi * HEADS_PER_GROUP, (gi + 1) * HEADS_PER_GROUP)
            o_g, lse_g = dilated_window_attention(q[:, :, hs], k[:, :, hs], v[:, :, hs],
                                                  window, dilation, slopes[hs])
            outs.append(o_g)
            lses.append(lse_g)
        w_grp = jax.nn.softmax(jnp.stack(lses), axis=0)
        o_attn = jnp.einsum('gbsh,gbshe->bshe', w_grp, jnp.stack(outs))
        y_attn = o_attn.reshape(B, S, ATTN_OUT_WIDTH).astype(x.dtype) @ w_branch_attn[l]

        u = causal_short_conv(cc * cx, conv_w[l])
        y_conv = (cb * u) @ w_branch_conv[l]

        ba, bb = jnp.split(b_gate[l], 2)
        merged = jax.nn.sigmoid(g_a + ba) * y_attn + jax.nn.sigmoid(g_b + bb) * y_conv
        x = x + gate1 * (merged @ w_out[l])

        h2 = rmsnorm(x, g_norm_mlp[l]) * (1.0 + scale2) + shift2
        x = x + gate2 * (jnp.square(jax.nn.relu(h2 @ w_mlp_in[l])) @ w_mlp_out[l])
    return rmsnorm(x, g_norm_final)
```

```python
import numpy as np
import concourse.bass as bass
import concourse.mybir as mybir
from concourse.alu_op_type import AluOpType as ALU
from concourse.bass_utils import run_bass_kernel_spmd

F32 = mybir.dt.float32
BF16 = mybir.dt.bfloat16
AF = mybir.ActivationFunctionType

D = 1024
S = 4096
NT = S // 128
KC = D // 128
DFF = 4096
IN_COLS = 9728
EPS = 1e-6
DIL = (1, 4, 16)
N_HEADS = 12
ENGS = ('pe', 'act', 'dve', 'pool', 'sp')
NEG = -30000.0
LEVEL = [5]
EVAC = ['act']
BL = [9]
NHEADS_RUN = [12]


class Op:
    __slots__ = ('eng', 'fn', 'cdeps', 'ddeps', 'is_dma', 'signal', 'rank', 'sem', 'cnt', 'prev',
                 'idx', 'bar')


class Prog:
    def __init__(self):
        self.ops = {e: [] for e in ENGS}
        self.W = {}
        self.R = {}
        self.barriers = []
        self.dma_since = []

    @staticmethod
    def _merge(dst, src):
        d, l = dst
        for e, o in src[0].items():
            if e not in d or d[e].idx < o.idx:
                d[e] = o
        for o in src[1]:
            l.append(o)

    @staticmethod
    def _single(op):
        return ({}, [op]) if op.is_dma else ({op.eng: op}, [])

    def add(self, eng, fn, reads=(), writes=(), is_dma=False, extra=()):
        op = Op()
        op.eng = eng
        op.fn = fn
        op.is_dma = is_dma
        op.signal = False
        op.rank = 0
        op.sem = None
        op.cnt = 0
        op.prev = None
        op.idx = len(self.ops[eng])
        op.bar = len(self.barriers) - 1
        deps = ({}, [])
        for k in reads:
            if k in self.W:
                self._merge(deps, self.W[k])
        for k in writes:
            if k in self.W:
                self._merge(deps, self.W[k])
            if k in self.R:
                self._merge(deps, self.R[k])
        for o in extra:
            if o is not None:
                self._merge(deps, self._single(o))
        op.cdeps = deps[0]
        seen = set()
        dd = []
        for o in deps[1]:
            if id(o) not in seen:
                seen.add(id(o))
                dd.append(o)
        op.ddeps = dd
        self.ops[eng].append(op)
        me = self._single(op)
        wset = set(writes)
        for k in writes:
            self.W[k] = (dict(me[0]), list(me[1]))
            self.R[k] = ({}, [])
        for k in reads:
            if k not in wset:
                if k not in self.R:
                    self.R[k] = ({}, [])
                self._merge(self.R[k], me)
        if is_dma:
            self.dma_since.append(op)
        return op

    def dma(self, q, fn, reads=(), writes=(), extra=()):
        return self.add(q, fn, reads, writes, is_dma=True, extra=extra)

    def barrier(self):
        c = {}
        for e in ENGS:
            for o in reversed(self.ops[e]):
                if not o.is_dma and o.fn is not None:
                    c[e] = o
                    break
        self.barriers.append((c, list(self.dma_since)))
        self.dma_since = []
        self.W = {}
        self.R = {}

    def emit(self, nc, nsem_dma=24):
        for e in ENGS:
            for op in self.ops[e]:
                for de, d in op.cdeps.items():
                    if de == 'pe' and e == 'pe' and not op.is_dma:
                        continue
                    d.signal = True
        for c, _ in self.barriers:
            for d in c.values():
                d.signal = True
        for e in ENGS:
            r = 0
            for op in self.ops[e]:
                if (not op.is_dma) and op.signal:
                    r += 1
                    op.rank = r
        psem = {e: nc.alloc_semaphore('prog_' + e) for e in ('pe', 'act', 'dve', 'pool')}
        for q in ('sp', 'pool'):
            nsem_dma = 24 if q == 'sp' else 6
            sems = [nc.alloc_semaphore('dma_%s_%d' % (q, i)) for i in range(nsem_dma)]
            last = [None] * nsem_dma
            n = 0
            for op in self.ops[q]:
                if op.is_dma:
                    i = n % nsem_dma
                    op.sem = sems[i]
                    op.prev = last[i]
                    op.cnt = (last[i].cnt if last[i] is not None else 0) + 16
                    last[i] = op
                    n += 1
        prog = self

        def emit_eng(e, E):
            known = {}
            cur_bar = -1

            def need(waits, d, consumer_is_pe_compute):
                if d.is_dma:
                    key, val = d.sem, d.cnt
                else:
                    if d.eng == 'pe' and consumer_is_pe_compute:
                        return
                    key, val = psem[d.eng], d.rank
                    assert val > 0
                k = id(key)
                if k not in waits or waits[k][1] < val:
                    waits[k] = (key, val)

            for op in prog.ops[e]:
                waits = {}
                pec = (e == 'pe' and not op.is_dma)
                while cur_bar < op.bar:
                    cur_bar += 1
                    c, dl = prog.barriers[cur_bar]
                    for d in c.values():
                        need(waits, d, False)
                    for d in dl:
                        need(waits, d, False)
                for d in op.cdeps.values():
                    need(waits, d, pec)
                for d in op.ddeps:
                    need(waits, d, pec)
                if op.is_dma and op.prev is not None:
                    need(waits, op.prev, False)
                for k, (sem, val) in waits.items():
                    if known.get(k, 0) >= val:
                        continue
                    E.wait_ge(sem, val)
                    known[k] = val
                if op.fn is None:
                    continue
                ins = op.fn(E)
                if op.is_dma:
                    ins.then_inc(op.sem, 16)
                elif op.signal:
                    ins.then_inc(psem[e], 1)

        with nc.Block() as block:
            @block.tensor
            def _(E):
                emit_eng('pe', E)

            @block.scalar
            def _(E):
                emit_eng('act', E)

            @block.vector
            def _(E):
                emit_eng('dve', E)

            @block.gpsimd
            def _(E):
                emit_eng('pool', E)

            @block.sync
            def _(E):
                emit_eng('sp', E)


class Mem:
    def __init__(self, nc, cap):
        self.t = nc.alloc_sbuf_tensor('big', [128, cap // 4], F32)
        self.top = 0
        self.cap = cap

    def alloc(self, nbytes):
        off = self.top
        self.top += (nbytes + 63) // 64 * 64
        assert self.top <= self.cap, (self.top, self.cap)
        return off

    def f32(self, nelem, parts=128):
        off = self.alloc(nelem * 4)
        return self.t[0:parts, off // 4: off // 4 + nelem]

    def bf16(self, nelem, parts=128):
        off = self.alloc(nelem * 2)
        return self.t[0:parts, off // 4: off // 4 + nelem // 2].bitcast(BF16)


def build_program(stop_after=None, dbg=(), nta=NT, skipA=()):
    nc = bass.Bass('TRN2', target_bir_lowering=False)
    P = Prog()

    def din(name, shape, dt=F32):
        return nc.dram_tensor(name, list(shape), dt, kind='ExternalInput').ap()

    x = din('x', [S, D])
    ccol = din('ccol', [128, 8])
    w_ada = din('w_ada', [D, 6 * D])
    b_ada = din('b_ada', [1, 6 * D])
    gvec = din('gvec', [1, 3 * D])
    w_in = din('w_in', [D, IN_COLS])
    bgate_d = din('bgate', [128, 16])
    convw_d = din('convw', [128, 24])
    w_ba = din('w_ba', [512, D])
    w_bc = din('w_bc', [D, D])
    w_out = din('w_out', [D, D])
    w1 = din('w1', [D, DFF])
    w2 = din('w2', [DFF, D])
    abias = din('abias', [N_HEADS, 128, 512])
    ident = din('ident', [128, 128])
    out = nc.dram_tensor('out', [S, D], F32, kind='ExternalOutput').ap()
    mrg = nc.dram_tensor('mrg', [16, 128, 8, 256], BF16, kind='Internal').ap()
    x1s = nc.dram_tensor('x1s', [S, D], F32, kind='Internal').ap()
    gbd = nc.dram_tensor('gbd', [3, 128, D], F32, kind='Internal').ap()
    w1b = nc.dram_tensor('w1b', [D, DFF], BF16, kind='Internal').ap()
    w2b = nc.dram_tensor('w2b', [DFF, D], BF16, kind='Internal').ap()
    dbg_out = {}

    mem = Mem(nc, 207 * 1024)
    ps = [nc.alloc_psum_tensor('ps%d' % b, [128, 512], F32) for b in range(8)]

    def psf(b):
        return ps[b][:, :]

    def psb(b):
        return ps[b][:, :].bitcast(BF16)

    def PS(b):
        return ('ps', b)

    store_ops = []

    def dump(name, ap, shape, dt, reads):
        if name not in dbg:
            return
        t = nc.dram_tensor('dbg_' + name, list(shape), dt, kind='ExternalOutput').ap()
        dbg_out[name] = t
        store_ops.append(P.dma('sp', lambda E, t=t, ap=ap: E.dma_start(out=t, in_=ap), reads=reads))

    idb = mem.bf16(128)
    onesb = mem.bf16(128)
    ones32 = mem.f32(128)
    modc = mem.f32(32)
    bgate = mem.f32(16)
    convw = mem.f32(24)
    c_sb = mem.f32(8)
    c_sig = mem.f32(8)
    cact = mem.f32(8)
    stat = mem.f32(4 * 64)
    mark_persist = mem.top

    hT = mem.bf16(KC * S).rearrange('p (k t) -> p k t', k=KC)
    mark_hT = mem.top

    def norm_stats(src, src_keys, i, xn, xnkey, junk):
        ssc = stat[:, 0 + (i % 64): 1 + (i % 64)]
        msc = stat[:, 64 + (i % 64): 65 + (i % 64)]
        rsc = stat[:, 128 + (i % 64): 129 + (i % 64)]
        rdc = stat[:, 192 + (i % 64): 193 + (i % 64)]
        ks = ('st', i % 64)
        P.add('act', lambda E: E.activation(out=junk, in_=src, func=AF.Square, accum_out=ssc),
              reads=src_keys, writes=['junk', ks])
        P.add('dve', lambda E: E.tensor_scalar(out=msc, in0=ssc, scalar1=1.0 / D, scalar2=EPS,
                                               op0=ALU.mult, op1=ALU.add), reads=[ks], writes=[ks])
        P.add('act', lambda E: E.activation(out=rsc, in_=msc, func=AF.Sqrt), reads=[ks], writes=[ks])
        P.add('dve', lambda E: E.reciprocal(out=rdc, in_=rsc), reads=[ks], writes=[ks])
        P.add('dve', lambda E: E.tensor_scalar(out=xn, in0=src, scalar1=rdc, scalar2=None, op0=ALU.mult),
              reads=list(src_keys) + [ks], writes=[xnkey])

    def norm_te(xn, xnkey, dst_fn, dstkeys_fn, col0, tpb, plain=False):
        tpv = psb(tpb).rearrange('p (k t) -> p k t', k=8)
        for k in range(8):
            P.add('pe', lambda E, k=k: E.transpose(out=tpv[:, k, :], in_=xn[:, k * 128:(k + 1) * 128],
                                                   identity=idb),
                  reads=[xnkey, 'idb'], writes=[PS(tpb)])
        if plain:
            allkeys = [kk for k in range(8) for kk in dstkeys_fn(k)]
            P.add('act', lambda E: E.activation(out=dst_fn(None), in_=tpv, func=AF.Copy),
                  reads=[PS(tpb)], writes=allkeys)
            return
        for k in range(8):
            gsc = modc[:, col0 + k: col0 + k + 1]
            shc = modc[:, col0 + 8 + k: col0 + 9 + k]
            P.add('act', lambda E, k=k, gsc=gsc, shc=shc: E.activation(
                out=dst_fn(k), in_=tpv[:, k, :], func=AF.Identity, bias=shc, scale=gsc),
                reads=[PS(tpb), 'modc'], writes=dstkeys_fn(k))

    xt = [mem.f32(D) for _ in range(3)]
    xnA = [mem.bf16(D) for _ in range(2)]
    junk = mem.bf16(D)
    def a_stats(i):
        sl = i % 3
        P.dma('sp', lambda E, i=i, sl=sl: E.dma_start(out=xt[sl], in_=x[i * 128:(i + 1) * 128, :]),
              writes=[('xt', sl)])
        norm_stats(xt[sl], [('xt', sl)], i, xnA[i % 2], ('xn', i % 2), junk)

    id32 = mem.f32(128)
    wst = [mem.f32(8 * 512).rearrange('p (k n) -> p k n', k=8) for _ in range(3)]
    modrow = mem.f32(6 * D, parts=1)
    grow = mem.f32(3 * D, parts=1)
    gsrow = mem.f32(2 * D, parts=1)
    gbt = [mem.f32(D) for _ in range(2)]

    P.dma('sp', lambda E: E.dma_start(out=c_sb, in_=ccol), writes=['c_sb'])
    P.dma('sp', lambda E: E.dma_start(out=id32, in_=ident), writes=['id32'])
    P.dma('sp', lambda E: E.dma_start(out=modrow, in_=b_ada), writes=['modrow'])
    P.dma('sp', lambda E: E.dma_start(out=grow, in_=gvec), writes=['grow'])
    P.dma('sp', lambda E: E.dma_start(out=bgate, in_=bgate_d), writes=['bgate'])
    P.dma('sp', lambda E: E.dma_start(out=convw, in_=convw_d), writes=['convw'])
    P.add('dve', lambda E: E.tensor_copy(out=idb, in_=id32), reads=['id32'], writes=['idb'])
    P.add('dve', lambda E: E.memset(onesb, 1.0), writes=['onesb'])
    P.add('dve', lambda E: E.memset(ones32, 1.0), writes=['ones32'])
    P.add('act', lambda E: E.activation(out=c_sig, in_=c_sb, func=AF.Sigmoid),
          reads=['c_sb'], writes=['c_sig'])
    P.add('dve', lambda E: E.tensor_tensor(out=cact, in0=c_sb, in1=c_sig, op=ALU.mult),
          reads=['c_sb', 'c_sig'], writes=['cact'])

    def p0_load(cg):
        sl = cg % 3
        P.dma('sp', lambda E, cg=cg, sl=sl: E.dma_start(
            out=wst[sl], in_=w_ada[:, cg * 512:(cg + 1) * 512].rearrange('(k p) n -> p k n', p=128)),
            writes=[('wst', sl)])

    def p0_group(cg):
        sl = cg % 3
        b = 6 + cg % 2
        for k in range(8):
            P.add('pe', lambda E, k=k, sl=sl, b=b: E.matmul(
                psf(b)[0:1, :], lhsT=cact[:, k:k + 1], rhs=wst[sl][:, k, :],
                start=(k == 0), stop=(k == 7)),
                reads=[('wst', sl), 'cact'], writes=[PS(b)])
        P.add('dve', lambda E, cg=cg, b=b: E.tensor_tensor(
            out=modrow[:, cg * 512:(cg + 1) * 512], in0=psf(b)[0:1, :],
            in1=modrow[:, cg * 512:(cg + 1) * 512], op=ALU.add),
            reads=[PS(b), 'modrow'], writes=['modrow'])
        if cg + 3 < 12:
            p0_load(cg + 3)

    cgn = 0
    a_stats(0)
    for cg in range(3):
        p0_load(cg)
    for i in range(nta):
        if i + 1 < nta:
            a_stats(i + 1)
        if i % 2 == 0 and cgn < 12:
            p0_group(cgn)
            cgn += 1
        norm_te(xnA[i % 2], ('xn', i % 2),
                lambda k, i=i: hT[:, :, i * 128:(i + 1) * 128] if k is None else hT[:, k, i * 128:(i + 1) * 128],
                lambda k, i=i: [('hT', i, k)], 0, i % 2, plain=True)
    while cgn < 12:
        p0_group(cgn)
        cgn += 1
    P.add('dve', lambda E: E.scalar_tensor_tensor(
        out=gsrow[:, 0:D], in0=modrow[:, D:2 * D], scalar=1.0, in1=grow[:, 0:D],
        op0=ALU.add, op1=ALU.mult), reads=['modrow', 'grow'], writes=['gsrow'])
    P.add('dve', lambda E: E.scalar_tensor_tensor(
        out=gsrow[:, D:2 * D], in0=modrow[:, 4 * D:5 * D], scalar=1.0, in1=grow[:, D:2 * D],
        op0=ALU.add, op1=ALU.mult), reads=['modrow', 'grow', 'gsrow'], writes=['gsrow'])
    colrows = [gsrow[:, 0:D], modrow[:, 0:D], gsrow[:, D:2 * D], modrow[:, 3 * D:4 * D]]
    for vi in range(4):
        for k in range(8):
            P.add('pe', lambda E, vi=vi, k=k: E.matmul(
                psf(2)[:, vi * 8 + k: vi * 8 + k + 1], lhsT=colrows[vi][:, k * 128:(k + 1) * 128],
                rhs=ones32[0:1, 0:1], start=True, stop=True),
                reads=['gsrow', 'modrow', 'ones32'], writes=[PS(2)])
    P.add('dve', lambda E: E.tensor_copy(out=modc, in_=psf(2)[:, 0:32]), reads=[PS(2)], writes=['modc'])
    for k in range(8):
        hk = [('hT', i, k) for i in range(NT)]
        gsc = modc[:, k:k + 1]
        shc = modc[:, 8 + k:9 + k]
        if k % 2 == 0:
            P.add('act', lambda E, k=k, gsc=gsc, shc=shc: E.activation(
                out=hT[:, k, :], in_=hT[:, k, :], func=AF.Identity, bias=shc, scale=gsc),
                reads=hk + ['modc'], writes=hk)
        else:
            P.add('pool', lambda E, k=k, gsc=gsc, shc=shc: E.tensor_scalar(
                out=hT[:, k, :], in0=hT[:, k, :], scalar1=gsc, scalar2=shc, op0=ALU.mult, op1=ALU.add),
                reads=hk + ['modc'], writes=hk)
    bcrows = [modrow[:, 2 * D:3 * D], modrow[:, 5 * D:6 * D], grow[:, 2 * D:3 * D]]
    for vi in range(3):
        sl = vi % 2
        for half in range(2):
            b = 3 + half
            P.add('pe', lambda E, vi=vi, half=half, b=b: E.matmul(
                psf(b), lhsT=ones32[0:1, 0:128], rhs=bcrows[vi][:, half * 512:(half + 1) * 512],
                start=True, stop=True), reads=['modrow', 'grow', 'ones32'], writes=[PS(b)])
            P.add('act', lambda E, sl=sl, half=half, b=b: E.activation(
                out=gbt[sl][:, half * 512:(half + 1) * 512], in_=psf(b), func=AF.Copy),
                reads=[PS(b)], writes=[('gbt', sl, half)])
        P.dma('sp', lambda E, vi=vi, sl=sl: E.dma_start(out=gbd[vi], in_=gbt[sl]),
              reads=[('gbt', sl, 0), ('gbt', sl, 1)], writes=[('gbd', vi)])
    dump('modc', modc, [128, 32], F32, ['modc'])
    hT_keys = lambda i0, i1: [('hT', i, k) for i in range(i0, i1) for k in range(8)]
    dump('hT', hT, [128, KC, S], BF16, hT_keys(0, NT))
    dump('stat', stat, [128, 256], F32, [('st', 0)])
    dump('xn', xnA[0], [128, D], BF16, [('xn', 0)])
    P.barrier()
    mem.top = mark_hT
    if stop_after == 1:
        return finish(nc, P, store_ops, dbg_out)

    oT = mem.bf16(4 * S).rearrange('p (j t) -> p j t', j=4)
    mark_oT = mem.top
    acc = mem.f32(2 * S).rearrange('p (c t) -> p c t', c=2)
    qTb = [mem.bf16(S) for _ in range(2)]
    kTb = [mem.bf16(S) for _ in range(2)]
    vtb = [mem.bf16(S).rearrange('p (b e) -> p b e', b=32) for _ in range(2)]
    vTh = mem.bf16(2048)
    wq = [mem.bf16(8 * 384).rearrange('p (k m n) -> p k m n', k=8, m=3) for _ in range(2)]
    bias4 = [mem.f32(512) for _ in range(2)]
    sbt = [mem.f32(512) for _ in range(2)]
    PT = [mem.bf16(512) for _ in range(2)]
    heads = [(j, g) for j in range(4) for g in range(3)]
    pbanks = [0, 1, 6, 7]
    bctr = {'p': 0}

    def load_w(hi):
        j, g = heads[hi]
        h = 4 * g + j
        ws = hi % 2
        for m in range(3):
            c0 = m * 1536 + h * 128
            P.dma('pool', lambda E, ws=ws, m=m, c0=c0: E.dma_start(
                out=wq[ws][:, :, m, :], in_=w_in[:, c0:c0 + 128].rearrange('(k p) n -> p k n', p=128)),
                writes=[('wq', ws, m)])

    def load_bias(hi):
        j, g = heads[hi]
        h = 4 * g + j
        ws = hi % 2
        P.dma('sp', lambda E, ws=ws, h=h: E.dma_start(out=bias4[ws], in_=abias[h]),
              writes=[('bias4', ws)])

    def proj_units(hi):
        j, g = heads[hi]
        d = DIL[g]
        nb = (S // d) // 128
        ws = hi % 2
        units = []
        for m, dstT, key in ((0, qTb[ws], ('qT', ws)), (1, kTb[ws], ('kT', ws))):
            dv = dstT.rearrange('p (r l) -> p r l', r=d)
            for tc in range(8):
                def u(m=m, dv=dv, key=key, tc=tc):
                    b = pbanks[bctr['p'] % 4]
                    bctr['p'] += 1
                    for k in range(8):
                        P.add('pe', lambda E, k=k: E.matmul(
                            psf(b), lhsT=wq[ws][:, k, m, :], rhs=hT[:, k, tc * 512:(tc + 1) * 512],
                            start=(k == 0), stop=(k == 7)),
                            reads=[('wq', ws, m)], writes=[PS(b)])
                    src = psf(b).rearrange('p (m r) -> p r m', r=d)
                    dst = dv[:, :, tc * (512 // d):(tc + 1) * (512 // d)]
                    if m == 0:
                        P.add('act', lambda E: E.activation(
                            out=dst, in_=src, func=AF.Copy, scale=float(128 ** -0.5)),
                            reads=[PS(b)], writes=[key])
                    elif g == 0:
                        P.add('dve', lambda E: E.tensor_copy(out=dst, in_=src), reads=[PS(b)], writes=[key])
                    else:
                        P.add('act', lambda E: E.activation(out=dst, in_=src, func=AF.Copy),
                              reads=[PS(b)], writes=[key])
                units.append(u)
        nbh = nb // 2
        vt4 = vtb[ws].rearrange('p (r n) e -> p r n e', r=d)
        for half in range(2):
            for c4 in range(4):
                def u(half=half, c4=c4):
                    b = pbanks[bctr['p'] % 4]
                    bctr['p'] += 1
                    tc = half * 4 + c4
                    for k in range(8):
                        P.add('pe', lambda E, k=k: E.matmul(
                            psf(b), lhsT=wq[ws][:, k, 2, :], rhs=hT[:, k, tc * 512:(tc + 1) * 512],
                            start=(k == 0), stop=(k == 7)),
                            reads=[('wq', ws, 2)], writes=[PS(b)])
                    dst = vTh[:, c4 * 512:(c4 + 1) * 512]
                    if g == 0 or c4 % 2 == 1:
                        P.add('dve', lambda E: E.tensor_copy(out=dst, in_=psf(b)),
                              reads=[PS(b)], writes=[('vTh', c4)])
                    else:
                        P.add('act', lambda E: E.activation(out=dst, in_=psf(b), func=AF.Copy),
                              reads=[PS(b)], writes=[('vTh', c4)])
                units.append(u)
            for q in range(2):
                def u(half=half, q=q):
                    b = pbanks[bctr['p'] % 4]
                    bctr['p'] += 1
                    if nbh >= 8:
                        r0, nr, n0, nn = 0, 1, nbh * half + 8 * q, 8
                    elif nbh == 4:
                        r0, nr, n0, nn = 2 * q, 2, nbh * half, 4
                    else:
                        r0, nr, n0, nn = 8 * q, 8, nbh * half, 1
                    tpv = psb(b).rearrange('p (a e) -> p a e', a=8)
                    wkeys = set()
                    idx = 0
                    for r in range(r0, r0 + nr):
                        for n in range(n0, n0 + nn):
                            st = r + d * 128 * (n - nbh * half)
                            wkeys.add(('vt', ws, (r * nb + n) // 4))
                            P.add('pe', lambda E, st=st, idx=idx: E.transpose(
                                out=tpv[:, idx, :], in_=vTh[:, st: st + d * 127 + 1: d], identity=idb),
                                reads=[('vTh', c) for c in range(4)] + ['idb'], writes=[PS(b)])
                            idx += 1
                    dstv = vt4[:, r0:r0 + nr, n0:n0 + nn, :]
                    srcv = psb(b).rearrange('p (r n e) -> p r n e', r=nr, n=nn)
                    P.add('act', lambda E: E.activation(out=dstv, in_=srcv, func=AF.Copy),
                          reads=[PS(b)], writes=sorted(wkeys))
                units.append(u)
        return units

    def att_scores(hi, pm):
        j, g = heads[hi]
        d = DIL[g]
        nb = (S // d) // 128
        ws = hi % 2
        qT, kT = qTb[ws], kTb[ws]
        qb0 = 2 * pm
        s0 = 1 if (qb0 % nb) == 0 else 0
        gp = hi * 16 + pm
        scb = 2 + (gp % 2)
        tsl = gp % 2
        rk = [('qT', ws), ('kT', ws)]
        if s0 == 0:
            P.add('pe', lambda E: E.matmul(
                psf(scb)[:, 0:128], lhsT=kT[:, (qb0 - 1) * 128:qb0 * 128], rhs=qT[:, qb0 * 128:(qb0 + 1) * 128],
                start=True, stop=True), reads=rk, writes=[PS(scb)])
        P.add('pe', lambda E: E.matmul(
            psf(scb)[:, 128:384], lhsT=kT[:, qb0 * 128:(qb0 + 1) * 128], rhs=qT[:, qb0 * 128:(qb0 + 2) * 128],
            start=True, stop=True), reads=rk, writes=[PS(scb)])
        P.add('pe', lambda E: E.matmul(
            psf(scb)[:, 384:512], lhsT=kT[:, (qb0 + 1) * 128:(qb0 + 2) * 128],
            rhs=qT[:, (qb0 + 1) * 128:(qb0 + 2) * 128],
            start=True, stop=True), reads=rk, writes=[PS(scb)])
        P.add('dve', lambda E: E.tensor_tensor(
            out=sbt[tsl][:, s0 * 128:512], in0=psf(scb)[:, s0 * 128:512],
            in1=bias4[ws][:, s0 * 128:512], op=ALU.add),
            reads=[PS(scb), ('bias4', ws)], writes=[('sbt', tsl)])
        P.add('act', lambda E: E.activation(
            out=PT[tsl][:, s0 * 128:512], in_=sbt[tsl][:, s0 * 128:512], func=AF.Exp),
            reads=[('sbt', tsl)], writes=[('PT', tsl)])

    def att_pv(hi, pm):
        j, g = heads[hi]
        d = DIL[g]
        nb = (S // d) // 128
        ws = hi % 2
        vt = vtb[ws]
        qb0 = 2 * pm
        s0 = 1 if (qb0 % nb) == 0 else 0
        gp = hi * 16 + pm
        pvb = 4 + (gp % 2)
        tsl = gp % 2
        pt = PT[tsl]
        pt4 = pt.rearrange('p (b t q) -> p b t q', b=2, t=2)
        rk = [('PT', tsl), 'onesb'] + [('vt', ws, kb // 4) for kb in (qb0 - 1, qb0, qb0 + 1) if kb >= 0]
        P.add('pe', lambda E: E.matmul(psf(pvb)[:, 0:256], lhsT=vt[:, qb0, :], rhs=pt[:, 128:384],
                                       start=True, stop=False), reads=rk, writes=[PS(pvb)])
        if s0 == 0:
            P.add('pe', lambda E: E.matmul(psf(pvb)[:, 0:128], lhsT=vt[:, qb0 - 1, :], rhs=pt[:, 0:128],
                                           start=False, stop=False), reads=rk, writes=[PS(pvb)])
        P.add('pe', lambda E: E.matmul(psf(pvb)[:, 128:256], lhsT=vt[:, qb0 + 1, :], rhs=pt[:, 384:512],
                                       start=False, stop=True), reads=rk, writes=[PS(pvb)])
        if s0 == 0:
            P.add('pe', lambda E: E.matmul(psf(pvb)[:, 256:512], lhsT=onesb, rhs=pt4[:, :, 0, :],
                                           start=True, stop=False), reads=rk, writes=[PS(pvb)])
            P.add('pe', lambda E: E.matmul(psf(pvb)[:, 256:512], lhsT=onesb, rhs=pt4[:, :, 1, :],
                                           start=False, stop=True), reads=rk, writes=[PS(pvb)])
        else:
            P.add('pe', lambda E: E.matmul(psf(pvb)[:, 256:512], lhsT=onesb, rhs=pt4[:, :, 1, :],
                                           start=True, stop=False), reads=rk, writes=[PS(pvb)])
            P.add('pe', lambda E: E.matmul(psf(pvb)[:, 384:512], lhsT=onesb, rhs=pt[:, 256:384],
                                           start=False, stop=True), reads=rk, writes=[PS(pvb)])
        r, n = divmod(qb0, nb)
        st0 = r + d * 128 * n
        accv = acc[:, :, st0: st0 + 255 * d + 1: d]
        pvv = psf(pvb).rearrange('p (c t) -> p c t', c=2)
        if g == 0:
            P.add('act', lambda E: E.activation(out=accv, in_=pvv, func=AF.Copy),
                  reads=[PS(pvb)], writes=[('acc', pm // 4)])
        else:
            aks = [('acc', q_) for q_ in range(4)]
            P.add('dve', lambda E: E.tensor_tensor(out=accv, in0=pvv, in1=accv, op=ALU.add),
                  reads=[PS(pvb)] + aks, writes=aks)

    def finalize(j):
        for q4 in range(4):
            sl_ = slice(q4 * 1024, (q4 + 1) * 1024)
            P.add('act', lambda E, sl_=sl_: E.activation(out=acc[:, 1, sl_], in_=acc[:, 1, sl_], func=AF.Ln),
                  reads=[('acc', q4)], writes=[('acc', q4)])
            P.add('act', lambda E, sl_=sl_: E.activation(out=acc[:, 1, sl_], in_=acc[:, 1, sl_], func=AF.Exp,
                                                         scale=-1.0), reads=[('acc', q4)], writes=[('acc', q4)])
            P.add('dve', lambda E, sl_=sl_, j=j: E.tensor_tensor(
                out=oT[:, j, sl_], in0=acc[:, 0, sl_], in1=acc[:, 1, sl_], op=ALU.mult),
                reads=[('acc', q4)], writes=[('oT', j, q4)])

    NHD = 12
    load_w(0)
    load_w(1)
    load_bias(0)
    for u in proj_units(0):
        u()
    precast = []
    for k in range(8):
        precast.append(lambda k=k: P.dma('pool', lambda E: E.dma_start(
            out=w1b[k * 128:(k + 1) * 128, :], in_=w1[k * 128:(k + 1) * 128, :]), writes=[('w1b', k)]))
    for k in range(8):
        precast.append(lambda k=k: P.dma('pool', lambda E: E.dma_start(
            out=w2b[k * 512:(k + 1) * 512, :].rearrange('(a p) n -> p a n', p=128),
            in_=w2[k * 512:(k + 1) * 512, :].rearrange('(a p) n -> p a n', p=128)), writes=[('w2b', k)]))
    for hi in range(NHD):
        if hi + 2 < NHD:
            load_w(hi + 2)
        if hi + 1 < NHD:
            load_bias(hi + 1)
        for _ in range(2):
            if precast:
                precast.pop(0)()
        units = proj_units(hi + 1) if hi + 1 < NHD else []
        ui = 0
        att_scores(hi, 0)
        for pm in range(16):
            if pm + 1 < 16:
                att_scores(hi, pm + 1)
            tgt = ((pm + 1) * len(units) + 15) // 16
            while ui < tgt:
                units[ui]()
                ui += 1
            att_pv(hi, pm)
        if hi % 3 == 2:
            finalize(hi // 3)
    dump('oT', oT, [128, 4, S], BF16, [('oT', j, q_) for j in range(4) for q_ in range(4)])
    P.barrier()
    mem.top = mark_oT
    if stop_after == 2:
        return finish(nc, P, store_ops, dbg_out)

    yT = mem.bf16(KC * S).rearrange('p (k t) -> p k t', k=KC)
    mark_yT = mem.top
    wc = [mem.bf16(8 * 384).rearrange('p (k m n) -> p k m n', k=8, m=3) for _ in range(3)]
    ccs = [mem.f32(512) for _ in range(2)]
    zb = [mem.f32(520) for _ in range(2)]
    ub = [mem.f32(512) for _ in range(2)]
    it = 0
    def c_load(c):
        ws = c % 3
        for m in range(3):
            c0 = 4608 + m * 1024 + c * 128
            P.dma('pool', lambda E, ws=ws, m=m, c0=c0: E.dma_start(
                out=wc[ws][:, :, m, :], in_=w_in[:, c0:c0 + 128].rearrange('(k p) n -> p k n', p=128)),
                writes=[('wc', ws, m)])

    c_load(0)
    c_load(1)
    for c in range(8):
        ws = c % 3
        if c + 2 < 8:
            c_load(c + 2)
        for tc in range(8):
            s = it % 2
            it += 1
            bb = 3 * s
            for m, b in ((1, bb), (2, bb + 1), (0, bb + 2)):
                for k in range(8):
                    P.add('pe', lambda E, ws=ws, m=m, k=k, tc=tc, b=b: E.matmul(
                        psf(b), lhsT=wc[ws][:, k, m, :], rhs=hT[:, k, tc * 512:(tc + 1) * 512],
                        start=(k == 0), stop=(k == 7)), reads=[('wc', ws, m)], writes=[PS(b)])
            P.add('act', lambda E, s=s, bb=bb: E.activation(out=ccs[s], in_=psf(bb), func=AF.Copy),
                  reads=[PS(bb)], writes=[('ccs', s)])
            P.add('dve', lambda E, s=s, bb=bb: E.tensor_tensor(
                out=zb[s][:, 2:514], in0=psf(bb + 1), in1=ccs[s], op=ALU.mult),
                reads=[PS(bb + 1), ('ccs', s)], writes=[('z', s)])
            if tc == 0:
                P.add('pool', lambda E, s=s: E.memset(zb[s][:, 0:2], 0.0), writes=[('z', s)], reads=[('z', s)])
            else:
                P.add('pool', lambda E, s=s: E.tensor_copy(out=zb[s][:, 0:2], in_=zb[1 - s][:, 512:514]),
                      reads=[('z', 1 - s), ('z', s)], writes=[('z', s)])
            w0 = convw[:, c * 3 + 0: c * 3 + 1]
            w1c = convw[:, c * 3 + 1: c * 3 + 2]
            w2c = convw[:, c * 3 + 2: c * 3 + 3]
            P.add('pool', lambda E, s=s, w0=w0: E.tensor_scalar(
                out=ub[s], in0=zb[s][:, 0:512], scalar1=w0, scalar2=0.0, op0=ALU.mult, op1=ALU.add),
                reads=[('z', s), 'convw'], writes=[('u', s)])
            P.add('dve', lambda E, s=s, w1c=w1c: E.scalar_tensor_tensor(
                out=ub[s], in0=zb[s][:, 1:513], scalar=w1c, in1=ub[s], op0=ALU.mult, op1=ALU.add),
                reads=[('z', s), ('u', s), 'convw'], writes=[('u', s)])
            P.add('dve', lambda E, s=s, w2c=w2c: E.scalar_tensor_tensor(
                out=ub[s], in0=zb[s][:, 2:514], scalar=w2c, in1=ub[s], op0=ALU.mult, op1=ALU.add),
                reads=[('z', s), ('u', s), 'convw'], writes=[('u', s)])
            P.add('dve', lambda E, s=s, bb=bb, c=c, tc=tc: E.tensor_tensor(
                out=yT[:, c, tc * 512:(tc + 1) * 512], in0=psf(bb + 2), in1=ub[s], op=ALU.mult),
                reads=[PS(bb + 2), ('u', s)], writes=[('yT', c, tc)])
    dump('yT', yT, [128, KC, S], BF16, [('yT', c, tc) for c in range(8) for tc in range(8)])
    P.barrier()
    mem.top = mark_yT
    if stop_after == 3:
        return finish(nc, P, store_ops, dbg_out)

    wd = [mem.bf16(28 * 128).rearrange('p (k n) -> p k n', k=28) for _ in range(3)]
    sat = [mem.f32(512) for _ in range(2)]
    sbt2 = [mem.f32(512) for _ in range(2)]
    tt = [mem.f32(512) for _ in range(2)]
    mout = [mem.bf16(512) for _ in range(3)]
    it = 0
    def d_load(f):
        ws = f % 3
        P.dma('pool', lambda E, ws=ws, f=f: E.dma_start(
            out=wd[ws][:, 0:4, :], in_=w_ba[:, f * 128:(f + 1) * 128].rearrange('(k p) n -> p k n', p=128)),
            writes=[('wd', ws, 0)])
        P.dma('pool', lambda E, ws=ws, f=f: E.dma_start(
            out=wd[ws][:, 4:12, :], in_=w_bc[:, f * 128:(f + 1) * 128].rearrange('(k p) n -> p k n', p=128)),
            writes=[('wd', ws, 1)])
        for m in range(2):
            c0 = 7680 + m * 1024 + f * 128
            P.dma('pool', lambda E, ws=ws, m=m, c0=c0: E.dma_start(
                out=wd[ws][:, 12 + 8 * m: 20 + 8 * m, :],
                in_=w_in[:, c0:c0 + 128].rearrange('(k p) n -> p k n', p=128)),
                writes=[('wd', ws, 2 + m)])

    d_load(0)
    d_load(1)
    for f in range(8):
        ws = f % 3
        if f + 2 < 8:
            d_load(f + 2)
        for tc in range(8):
            s = it % 2
            mo = it % 3
            it += 1
            b0 = 4 * s
            tsl = slice(tc * 512, (tc + 1) * 512)
            for k in range(4):
                P.add('pe', lambda E, ws=ws, k=k, tsl=tsl, b0=b0: E.matmul(
                    psf(b0), lhsT=wd[ws][:, k, :], rhs=oT[:, k, tsl], start=(k == 0), stop=(k == 3)),
                    reads=[('wd', ws, 0)], writes=[PS(b0)])
            for k in range(8):
                P.add('pe', lambda E, ws=ws, k=k, tsl=tsl, b0=b0: E.matmul(
                    psf(b0 + 1), lhsT=wd[ws][:, 12 + k, :], rhs=hT[:, k, tsl], start=(k == 0), stop=(k == 7)),
                    reads=[('wd', ws, 2)], writes=[PS(b0 + 1)])
            for k in range(8):
                P.add('pe', lambda E, ws=ws, k=k, tsl=tsl, b0=b0: E.matmul(
                    psf(b0 + 2), lhsT=wd[ws][:, 4 + k, :], rhs=yT[:, k, tsl], start=(k == 0), stop=(k == 7)),
                    reads=[('wd', ws, 1)], writes=[PS(b0 + 2)])
            for k in range(8):
                P.add('pe', lambda E, ws=ws, k=k, tsl=tsl, b0=b0: E.matmul(
                    psf(b0 + 3), lhsT=wd[ws][:, 20 + k, :], rhs=hT[:, k, tsl], start=(k == 0), stop=(k == 7)),
                    reads=[('wd', ws, 3)], writes=[PS(b0 + 3)])
            P.add('act', lambda E, s=s, b0=b0, f=f: E.activation(
                out=sat[s], in_=psf(b0 + 1), func=AF.Sigmoid, bias=bgate[:, f:f + 1]),
                reads=[PS(b0 + 1), 'bgate'], writes=[('sat', s)])
            P.add('act', lambda E, s=s, b0=b0, f=f: E.activation(
                out=sbt2[s], in_=psf(b0 + 3), func=AF.Sigmoid, bias=bgate[:, 8 + f:9 + f]),
                reads=[PS(b0 + 3), 'bgate'], writes=[('sbt2', s)])
            P.add('dve', lambda E, s=s, b0=b0: E.tensor_tensor(out=tt[s], in0=psf(b0), in1=sat[s], op=ALU.mult),
                  reads=[PS(b0), ('sat', s)], writes=[('tt', s)])
            P.add('dve', lambda E, s=s, b0=b0: E.tensor_tensor(out=sbt2[s], in0=psf(b0 + 2), in1=sbt2[s],
                                                               op=ALU.mult),
                  reads=[PS(b0 + 2), ('sbt2', s)], writes=[('sbt2', s)])
            P.add('pool', lambda E, s=s, mo=mo: E.tensor_tensor(out=mout[mo], in0=tt[s], in1=sbt2[s], op=ALU.add),
                  reads=[('tt', s), ('sbt2', s)], writes=[('mout', mo)])
            P.dma('sp', lambda E, mo=mo, f=f, tc=tc: E.dma_start(
                out=mrg[2 * tc:2 * tc + 2, :, f, :].rearrange('c p t -> p c t'),
                in_=mout[mo].rearrange('p (c t) -> p c t', c=2)),
                  reads=[('mout', mo)], writes=[('mrg', f, tc)])
    if 'mrg' in dbg:
        t = nc.dram_tensor('dbg_mrg', [16, 128, 8, 256], BF16, kind='ExternalOutput').ap()
        dbg_out['mrg'] = t
        store_ops.append(P.dma('sp', lambda E, t=t: E.dma_start(out=t, in_=mrg),
                               reads=[('mrg', f, tc) for f in range(8) for tc in range(8)]))
    P.barrier()
    mem.top = mark_persist
    if stop_after == 4:
        return finish(nc, P, store_ops, dbg_out)

    w1s = mem.bf16(KC * DFF).rearrange('p (k n) -> p k n', k=KC)
    w2s = mem.bf16(32 * D).rearrange('p (k n) -> p k n', k=32)
    mark_F = mem.top
    wo = mem.bf16(KC * D).rearrange('p (k n) -> p k n', k=KC)
    g1b = mem.f32(D)
    mrgc = [mem.bf16(8 * 256).rearrange('p (f t) -> p f t', f=8) for _ in range(4)]
    xe = [mem.f32(2 * D).rearrange('p (i n) -> p i n', i=2) for _ in range(4)]
    tmpe = [mem.f32(512) for _ in range(2)]
    for k in range(8):
        P.dma('pool', lambda E, k=k: E.dma_start(out=wo[:, k, :], in_=w_out[k * 128:(k + 1) * 128, :]),
              writes=[('wo', k)])
    P.dma('sp', lambda E: E.dma_start(out=g1b, in_=gbd[0]), writes=['g1b'])
    ectr = {'it': 0}

    def e_load(c):
        s = c % 4
        tsl = slice(c * 256, (c + 1) * 256)
        P.dma('sp', lambda E: E.dma_start(
            out=mrgc[s], in_=mrg[c]), writes=[('mrgc', s)])

    def e_compute(c):
        s = c % 4
        tsl = slice(c * 256, (c + 1) * 256)
        for i in range(2):
            for half in range(2):
                b = ectr['it'] % 4
                ts_ = ectr['it'] % 2
                ectr['it'] += 1
                hs = slice(half * 512, (half + 1) * 512)
                for f in range(8):
                    P.add('pe', lambda E, i=i, f=f, hs=hs, b=b: E.matmul(
                        psf(b), lhsT=mrgc[s][:, f, i * 128:(i + 1) * 128], rhs=wo[:, f, hs],
                        start=(f == 0), stop=(f == 7)),
                        reads=[('mrgc', s)] + [('wo', f)], writes=[PS(b)])
                P.add('dve', lambda E, i=i, b=b, hs=hs: E.tensor_tensor(
                    out=xe[s][:, i, hs], in0=psf(b), in1=g1b[:, hs], op=ALU.mult),
                    reads=[PS(b), 'g1b'], writes=[('xe', s, i, hs.start)])
        P.dma('sp', lambda E: E.dma_start(
            out=x1s[tsl, :].rearrange('(i p) n -> p i n', p=128), in_=xe[s]),
            reads=[('xe', s, i, h0) for i in range(2) for h0 in (0, 512)], writes=[('x1s', c)])

    for c in range(4):
        e_load(c)
    wpieces = []
    for k in range(8):
        wpieces.append(lambda k=k: P.dma('sp', lambda E: E.dma_start(
            out=w1s[:, k, :], in_=w1b[k * 128:(k + 1) * 128, :]), writes=[('w1s', k)]))
    for k4 in range(8):
        wpieces.append(lambda k4=k4: P.dma('sp', lambda E: E.dma_start(
            out=w2s[:, k4 * 4:(k4 + 1) * 4, :],
            in_=w2b[k4 * 512:(k4 + 1) * 512, :].rearrange('(a p) n -> p a n', p=128)),
            writes=[('w2s', k4 * 4 + a) for a in range(4)]))
    for _ in range(2):
        wpieces.pop(0)()
    for c in range(16):
        e_compute(c)
        if c + 4 < 16:
            e_load(c + 4)
        if wpieces:
            wpieces.pop(0)()
    while wpieces:
        wpieces.pop(0)()
    if 'x1' in dbg:
        t = nc.dram_tensor('dbg_x1', [S, D], F32, kind='ExternalOutput').ap()
        dbg_out['x1'] = t
        store_ops.append(P.dma('sp', lambda E, t=t: E.dma_start(out=t, in_=x1s), reads=[('x1s', c) for c in range(16)]))
    P.barrier()
    mem.top = mark_F
    if stop_after == 5:
        return finish(nc, P, store_ops, dbg_out)

    fT = mem.bf16(32 * 256).rearrange('p (j t) -> p j t', j=32)
    g2b = mem.f32(D)
    gfb = mem.f32(D)
    x1t = [mem.f32(2 * D).rearrange('p (i n) -> p i n', i=2) for _ in range(2)]
    dtmp = mem.f32(2 * D).rearrange('p (i n) -> p i n', i=2)
    h2T = [mem.bf16(KC * 256).rearrange('p (k t) -> p k t', k=KC) for _ in range(2)]
    xnF = mem.bf16(D)
    junkF = mem.bf16(D)
    rl = [mem.f32(512) for _ in range(2)]
    tmpf = [mem.f32(512) for _ in range(2)]
    P.dma('sp', lambda E: E.dma_start(out=g2b, in_=gbd[1]), writes=['g2b'])
    P.dma('sp', lambda E: E.dma_start(out=gfb, in_=gbd[2]), writes=['gfb'])
    xnF2 = [xnF, mem.bf16(D)]
    ctr = {'mi': 0, 'mo': 0, 'sti': 0}

    def f_load_stats(ch):
        s = ch % 2
        rows = slice(ch * 256, (ch + 1) * 256)
        P.dma('sp', lambda E, s=s, rows=rows: E.dma_start(
            out=x1t[s], in_=x[rows, :].rearrange('(i p) n -> p i n', p=128)),
            writes=[('x1t', s, 0), ('x1t', s, 1)])
        P.dma('sp', lambda E, rows=rows: E.dma_start(
            out=dtmp, in_=x1s[rows, :].rearrange('(i p) n -> p i n', p=128)), writes=['dtmp'])
        for i in range(2):
            P.add('pool', lambda E, s=s, i=i: E.tensor_tensor(
                out=x1t[s][:, i, :], in0=dtmp[:, i, :], in1=x1t[s][:, i, :], op=ALU.add),
                reads=['dtmp', ('x1t', s, i)], writes=[('x1t', s, i)])
        for i in range(2):
            norm_stats(x1t[s][:, i, :], [('x1t', s, i)], ctr['sti'], xnF2[i], ('xnF', i), junkF)
            ctr['sti'] += 1

    def f_te(ch):
        s = ch % 2
        for i in range(2):
            norm_te(xnF2[i], ('xnF', i), lambda k, s=s, i=i: h2T[s][:, k, i * 128:(i + 1) * 128],
                    lambda k, s=s, i=i: [('h2T', s, i, k)], 16, i % 2)

    def f_mlp_in(ch):
        s = ch % 2
        h2keys = [('h2T', s, i, k) for i in range(2) for k in range(8)]
        for jp in range(16):
            b = 2 + (ctr['mi'] % 3)
            rs_ = ctr['mi'] % 2
            ctr['mi'] += 1
            for jj in range(2):
                jf = jp * 2 + jj
                for k in range(8):
                    P.add('pe', lambda E, s=s, jf=jf, jj=jj, k=k, b=b: E.matmul(
                        psf(b)[:, jj * 256:(jj + 1) * 256], lhsT=w1s[:, k, jf * 128:(jf + 1) * 128],
                        rhs=h2T[s][:, k, :], start=(k == 0), stop=(k == 7)),
                        reads=h2keys + [('w1s', k)], writes=[PS(b)])
            P.add('act', lambda E, b=b, rs_=rs_: E.activation(out=rl[rs_], in_=psf(b), func=AF.Relu),
                  reads=[PS(b)], writes=[('rl', rs_)])
            P.add('dve', lambda E, b=b, rs_=rs_, jp=jp: E.tensor_tensor(
                out=fT[:, jp * 2:jp * 2 + 2, :], in0=psf(b).rearrange('p (j t) -> p j t', j=2),
                in1=rl[rs_].rearrange('p (j t) -> p j t', j=2), op=ALU.mult),
                reads=[PS(b), ('rl', rs_)], writes=[('fT', jp)])

    def f_mlp_out(ch, i):
        s = ch % 2
        for half in range(2):
            b = 5 + (ctr['mo'] % 3)
            ts_ = ctr['mo'] % 2
            ctr['mo'] += 1
            hs = slice(half * 512, (half + 1) * 512)
            for jf in range(32):
                P.add('pe', lambda E, jf=jf, i=i, hs=hs, b=b: E.matmul(
                    psf(b), lhsT=fT[:, jf, i * 128:(i + 1) * 128], rhs=w2s[:, jf, hs],
                    start=(jf == 0), stop=(jf == 31)),
                    reads=[('fT', jf // 2), ('w2s', jf)], writes=[PS(b)])
            P.add('dve', lambda E, b=b, ts_=ts_, hs=hs: E.tensor_tensor(
                out=tmpf[ts_], in0=psf(b), in1=g2b[:, hs], op=ALU.mult),
                reads=[PS(b), 'g2b'], writes=[('tmpf', ts_)])
            P.add('pool', lambda E, s=s, i=i, ts_=ts_, hs=hs: E.tensor_tensor(
                out=x1t[s][:, i, hs], in0=tmpf[ts_], in1=x1t[s][:, i, hs], op=ALU.add),
                reads=[('tmpf', ts_), ('x1t', s, i)], writes=[('x1t', s, i)])
        src = x1t[s][:, i, :]
        ci = ctr['sti'] % 64
        ctr['sti'] += 1
        ssc = stat[:, ci:ci + 1]
        msc = stat[:, 64 + ci:65 + ci]
        rsc = stat[:, 128 + ci:129 + ci]
        rdc = stat[:, 192 + ci:193 + ci]
        ks = ('st', ci)
        P.add('act', lambda E, src=src, ssc=ssc: E.activation(out=junkF, in_=src, func=AF.Square, accum_out=ssc),
              reads=[('x1t', s, i)], writes=['junk', ks])
        P.add('dve', lambda E, ssc=ssc, msc=msc: E.tensor_scalar(
            out=msc, in0=ssc, scalar1=1.0 / D, scalar2=EPS, op0=ALU.mult, op1=ALU.add), reads=[ks], writes=[ks])
        P.add('act', lambda E, rsc=rsc, msc=msc: E.activation(out=rsc, in_=msc, func=AF.Sqrt),
              reads=[ks], writes=[ks])
        P.add('dve', lambda E, rsc=rsc, rdc=rdc: E.reciprocal(out=rdc, in_=rsc), reads=[ks], writes=[ks])
        P.add('dve', lambda E, src=src, rdc=rdc: E.scalar_tensor_tensor(
            out=src, in0=src, scalar=rdc, in1=gfb, op0=ALU.mult, op1=ALU.mult),
            reads=[('x1t', s, i), ks, 'gfb'], writes=[('x1t', s, i)])

    def f_store(ch):
        s = ch % 2
        rows = slice(ch * 256, (ch + 1) * 256)
        store_ops.append(P.dma('sp', lambda E, s=s, rows=rows: E.dma_start(
            out=out[rows, :].rearrange('(i p) n -> p i n', p=128), in_=x1t[s]),
            reads=[('x1t', s, 0), ('x1t', s, 1)], writes=[('out', ch)]))

    f_load_stats(0)
    f_te(0)
    for ch in range(16):
        f_mlp_in(ch)
        if ch + 1 < 16:
            f_load_stats(ch + 1)
        f_mlp_out(ch, 0)
        if ch + 1 < 16:
            f_te(ch + 1)
        f_mlp_out(ch, 1)
        f_store(ch)
    return finish(nc, P, store_ops, dbg_out)


def finish(nc, P, store_ops, dbg_out):
    P.add('sp', None, extra=store_ops)
    P.emit(nc)
    return nc, dbg_out


def _alibi_bias_table():
    slopes = 2.0 ** (-8.0 * np.arange(1, N_HEADS + 1, dtype=np.float64) / N_HEADS)
    kj = np.arange(128)[:, None]
    qi = np.arange(128)[None, :]
    tab = np.zeros((N_HEADS, 128, 4, 128), np.float32)
    for h in range(N_HEADS):
        d = DIL[h // 4]
        c = slopes[h] * d
        dprev = 128 + qi - kj
        prev = np.where(dprev <= 128, -c * dprev, NEG)
        dcur = qi - kj
        cur = np.where(dcur >= 0, -c * dcur, NEG)
        for b in range(2):
            tab[h, :, 2 * b + 0, :] = prev
            tab[h, :, 2 * b + 1, :] = cur
    return tab.reshape(N_HEADS, 128, 512)


def make_in_maps(inputs, cores):
    f = lambda a: np.ascontiguousarray(np.asarray(a, dtype=np.float32))
    x = f(inputs['x'])
    c = f(inputs['c'])
    shared = {
        'w_ada': f(inputs['w_ada'][0]),
        'b_ada': f(inputs['b_ada'][0]).reshape(1, -1),
        'gvec': np.concatenate([f(inputs['g_norm_mix'][0]), f(inputs['g_norm_mlp'][0]),
                                f(inputs['g_norm_final'])]).reshape(1, -1),
        'w_in': f(inputs['w_in'][0]),
        'bgate': f(f(inputs['b_gate'][0]).reshape(16, 128).T),
        'convw': f(f(inputs['conv_w'][0]).T.reshape(8, 128, 3).transpose(1, 0, 2).reshape(128, 24)),
        'w_ba': f(inputs['w_branch_attn'][0]),
        'w_bc': f(inputs['w_branch_conv'][0]),
        'w_out': f(inputs['w_out'][0]),
        'w1': f(inputs['w_mlp_in'][0]),
        'w2': f(inputs['w_mlp_out'][0]),
        'abias': _alibi_bias_table(),
        'ident': np.eye(128, dtype=np.float32),
    }
    maps = []
    for b in cores:
        m = dict(shared)
        m['x'] = f(x[b])
        m['ccol'] = f(c[b].reshape(8, 128).T)
        maps.append(m)
    return maps


_CACHE = {}


def kernel(**inputs):
    if 'nc' not in _CACHE:
        _CACHE['nc'] = build_program()[0]
    nc = _CACHE['nc']
    cores = list(range(8))
    in_maps = make_in_maps(inputs, cores)
    res = run_bass_kernel_spmd(nc, in_maps, core_ids=cores)
    return np.stack([np.asarray(r['out'], dtype=np.float32) for r in res.results], axis=0)
```

```python
import numpy as np
import concourse.bass as bass
import concourse.mybir as mybir
from concourse.alu_op_type import AluOpType as ALU
from concourse.bass_utils import run_bass_kernel_spmd

F32 = mybir.dt.float32
BF16 = mybir.dt.bfloat16
AF = mybir.ActivationFunctionType

D = 1024
S = 4096
NT = S // 128
KC = D // 128
DFF = 4096
IN_COLS = 9728
EPS = 1e-6
DIL = (1, 4, 16)
N_HEADS = 12
ENGS = ('pe', 'act', 'dve', 'pool', 'sp')
NEG = -30000.0
LEVEL = [5]
EVAC = ['act']
BL = [9]
NHEADS_RUN = [12]


class Op:
    __slots__ = ('eng', 'fn', 'cdeps', 'ddeps', 'is_dma', 'signal', 'rank', 'sem', 'cnt', 'prev',
                 'idx', 'bar')


class Prog:
    def __init__(self):
        self.ops = {e: [] for e in ENGS}
        self.W = {}
        self.R = {}
        self.barriers = []
        self.dma_since = []

    @staticmethod
    def _merge(dst, src):
        d, l = dst
        for e, o in src[0].items():
            if e not in d or d[e].idx < o.idx:
                d[e] = o
        for o in src[1]:
            l.append(o)

    @staticmethod
    def _single(op):
        return ({}, [op]) if op.is_dma else ({op.eng: op}, [])

    def add(self, eng, fn, reads=(), writes=(), is_dma=False, extra=()):
        op = Op()
        op.eng = eng
        op.fn = fn
        op.is_dma = is_dma
        op.signal = False
        op.rank = 0
        op.sem = None
        op.cnt = 0
        op.prev = None
        op.idx = len(self.ops[eng])
        op.bar = len(self.barriers) - 1
        deps = ({}, [])
        for k in reads:
            if k in self.W:
                self._merge(deps, self.W[k])
        for k in writes:
            if k in self.W:
                self._merge(deps, self.W[k])
            if k in self.R:
                self._merge(deps, self.R[k])
        for o in extra:
            if o is not None:
                self._merge(deps, self._single(o))
        op.cdeps = deps[0]
        seen = set()
        dd = []
        for o in deps[1]:
            if id(o) not in seen:
                seen.add(id(o))
                dd.append(o)
        op.ddeps = dd
        self.ops[eng].append(op)
        me = self._single(op)
        wset = set(writes)
        for k in writes:
            self.W[k] = (dict(me[0]), list(me[1]))
            self.R[k] = ({}, [])
        for k in reads:
            if k not in wset:
                if k not in self.R:
                    self.R[k] = ({}, [])
                self._merge(self.R[k], me)
        if is_dma:
            self.dma_since.append(op)
        return op

    def dma(self, q, fn, reads=(), writes=(), extra=()):
        return self.add(q, fn, reads, writes, is_dma=True, extra=extra)

    def barrier(self):
        c = {}
        for e in ENGS:
            for o in reversed(self.ops[e]):
                if not o.is_dma and o.fn is not None:
                    c[e] = o
                    break
        self.barriers.append((c, list(self.dma_since)))
        self.dma_since = []
        self.W = {}
        self.R = {}

    def emit(self, nc, nsem_dma=24):
        for e in ENGS:
            for op in self.ops[e]:
                for de, d in op.cdeps.items():
                    if de == 'pe' and e == 'pe' and not op.is_dma:
                        continue
                    d.signal = True
        for c, _ in self.barriers:
            for d in c.values():
                d.signal = True
        for e in ENGS:
            r = 0
            for op in self.ops[e]:
                if (not op.is_dma) and op.signal:
                    r += 1
                    op.rank = r
        psem = {e: nc.alloc_semaphore('prog_' + e) for e in ('pe', 'act', 'dve', 'pool')}
        for q in ('sp', 'pool'):
            nsem_dma = 24 if q == 'sp' else 6
            sems = [nc.alloc_semaphore('dma_%s_%d' % (q, i)) for i in range(nsem_dma)]
            last = [None] * nsem_dma
            n = 0
            for op in self.ops[q]:
                if op.is_dma:
                    i = n % nsem_dma
                    op.sem = sems[i]
                    op.prev = last[i]
                    op.cnt = (last[i].cnt if last[i] is not None else 0) + 16
                    last[i] = op
                    n += 1
        prog = self

        def emit_eng(e, E):
            known = {}
            cur_bar = -1

            def need(waits, d, consumer_is_pe_compute):
                if d.is_dma:
                    key, val = d.sem, d.cnt
                else:
                    if d.eng == 'pe' and consumer_is_pe_compute:
                        return
                    key, val = psem[d.eng], d.rank
                    assert val > 0
                k = id(key)
                if k not in waits or waits[k][1] < val:
                    waits[k] = (key, val)

            for op in prog.ops[e]:
                waits = {}
                pec = (e == 'pe' and not op.is_dma)
                while cur_bar < op.bar:
                    cur_bar += 1
                    c, dl = prog.barriers[cur_bar]
                    for d in c.values():
                        need(waits, d, False)
                    for d in dl:
                        need(waits, d, False)
                for d in op.cdeps.values():
                    need(waits, d, pec)
                for d in op.ddeps:
                    need(waits, d, pec)
                if op.is_dma and op.prev is not None:
                    need(waits, op.prev, False)
                for k, (sem, val) in waits.items():
                    if known.get(k, 0) >= val:
                        continue
                    E.wait_ge(sem, val)
                    known[k] = val
                if op.fn is None:
                    continue
                ins = op.fn(E)
                if op.is_dma:
                    ins.then_inc(op.sem, 16)
                elif op.signal:
                    ins.then_inc(psem[e], 1)

        with nc.Block() as block:
            @block.tensor
            def _(E):
                emit_eng('pe', E)

            @block.scalar
            def _(E):
                emit_eng('act', E)

            @block.vector
            def _(E):
                emit_eng('dve', E)

            @block.gpsimd
            def _(E):
                emit_eng('pool', E)

            @block.sync
            def _(E):
                emit_eng('sp', E)


class Mem:
    def __init__(self, nc, cap):
        self.t = nc.alloc_sbuf_tensor('big', [128, cap // 4], F32)
        self.top = 0
        self.cap = cap

    def alloc(self, nbytes):
        off = self.top
        self.top += (nbytes + 63) // 64 * 64
        assert self.top <= self.cap, (self.top, self.cap)
        return off

    def f32(self, nelem, parts=128):
        off = self.alloc(nelem * 4)
        return self.t[0:parts, off // 4: off // 4 + nelem]

    def bf16(self, nelem, parts=128):
        off = self.alloc(nelem * 2)
        return self.t[0:parts, off // 4: off // 4 + nelem // 2].bitcast(BF16)


def build_program(stop_after=None, dbg=(), nta=NT, skipA=()):
    nc = bass.Bass('TRN2', target_bir_lowering=False)
    P = Prog()

    def din(name, shape, dt=F32):
        return nc.dram_tensor(name, list(shape), dt, kind='ExternalInput').ap()

    x = din('x', [S, D])
    ccol = din('ccol', [128, 8])
    w_ada = din('w_ada', [D, 6 * D])
    b_ada = din('b_ada', [1, 6 * D])
    gvec = din('gvec', [1, 3 * D])
    w_in = din('w_in', [D, IN_COLS])
    bgate_d = din('bgate', [128, 16])
    convw_d = din('convw', [128, 24])
    w_ba = din('w_ba', [512, D])
    w_bc = din('w_bc', [D, D])
    w_out = din('w_out', [D, D])
    w1 = din('w1', [D, DFF])
    w2 = din('w2', [DFF, D])
    abias = din('abias', [N_HEADS, 128, 512])
    ident = din('ident', [128, 128])
    out = nc.dram_tensor('out', [S, D], F32, kind='ExternalOutput').ap()
    mrg = nc.dram_tensor('mrg', [16, 128, 8, 256], BF16, kind='Internal').ap()
    x1s = nc.dram_tensor('x1s', [S, D], F32, kind='Internal').ap()
    gbd = nc.dram_tensor('gbd', [3, 128, D], F32, kind='Internal').ap()
    w1b = nc.dram_tensor('w1b', [D, DFF], BF16, kind='Internal').ap()
    w2b = nc.dram_tensor('w2b', [DFF, D], BF16, kind='Internal').ap()
    dbg_out = {}

    mem = Mem(nc, 207 * 1024)
    ps = [nc.alloc_psum_tensor('ps%d' % b, [128, 512], F32) for b in range(8)]

    def psf(b):
        return ps[b][:, :]

    def psb(b):
        return ps[b][:, :].bitcast(BF16)

    def PS(b):
        return ('ps', b)

    store_ops = []

    def dump(name, ap, shape, dt, reads):
        if name not in dbg:
            return
        t = nc.dram_tensor('dbg_' + name, list(shape), dt, kind='ExternalOutput').ap()
        dbg_out[name] = t
        store_ops.append(P.dma('sp', lambda E, t=t, ap=ap: E.dma_start(out=t, in_=ap), reads=reads))

    idb = mem.bf16(128)
    onesb = mem.bf16(128)
    ones32 = mem.f32(128)
    modc = mem.f32(32)
    bgate = mem.f32(16)
    convw = mem.f32(24)
    c_sb = mem.f32(8)
    c_sig = mem.f32(8)
    cact = mem.f32(8)
    stat = mem.f32(4 * 64)
    mark_persist = mem.top

    hT = mem.bf16(KC * S).rearrange('p (k t) -> p k t', k=KC)
    mark_hT = mem.top

    def norm_stats(src, src_keys, i, xn, xnkey, junk):
        ssc = stat[:, 0 + (i % 64): 1 + (i % 64)]
        msc = stat[:, 64 + (i % 64): 65 + (i % 64)]
        rsc = stat[:, 128 + (i % 64): 129 + (i % 64)]
        rdc = stat[:, 192 + (i % 64): 193 + (i % 64)]
        ks = ('st', i % 64)
        P.add('act', lambda E: E.activation(out=junk, in_=src, func=AF.Square, accum_out=ssc),
              reads=src_keys, writes=['junk', ks])
        P.add('dve', lambda E: E.tensor_scalar(out=msc, in0=ssc, scalar1=1.0 / D, scalar2=EPS,
                                               op0=ALU.mult, op1=ALU.add), reads=[ks], writes=[ks])
        P.add('act', lambda E: E.activation(out=rsc, in_=msc, func=AF.Sqrt), reads=[ks], writes=[ks])
        P.add('dve', lambda E: E.reciprocal(out=rdc, in_=rsc), reads=[ks], writes=[ks])
        P.add('dve', lambda E: E.tensor_scalar(out=xn, in0=src, scalar1=rdc, scalar2=None, op0=ALU.mult),
              reads=list(src_keys) + [ks], writes=[xnkey])

    def norm_te(xn, xnkey, dst_fn, dstkeys_fn, col0, tpb, plain=False):
        tpv = psb(tpb).rearrange('p (k t) -> p k t', k=8)
        for k in range(8):
            P.add('pe', lambda E, k=k: E.transpose(out=tpv[:, k, :], in_=xn[:, k * 128:(k + 1) * 128],
                                                   identity=idb),
                  reads=[xnkey, 'idb'], writes=[PS(tpb)])
        if plain:
            allkeys = [kk for k in range(8) for kk in dstkeys_fn(k)]
            P.add('act', lambda E: E.activation(out=dst_fn(None), in_=tpv, func=AF.Copy),
                  reads=[PS(tpb)], writes=allkeys)
            return
        for k in range(8):
            gsc = modc[:, col0 + k: col0 + k + 1]
            shc = modc[:, col0 + 8 + k: col0 + 9 + k]
            P.add('act', lambda E, k=k, gsc=gsc, shc=shc: E.activation(
                out=dst_fn(k), in_=tpv[:, k, :], func=AF.Identity, bias=shc, scale=gsc),
                reads=[PS(tpb), 'modc'], writes=dstkeys_fn(k))

    xt = [mem.f32(D) for _ in range(3)]
    xnA = [mem.bf16(D) for _ in range(2)]
    junk = mem.bf16(D)
    def a_stats(i):
        sl = i % 3
        P.dma('sp', lambda E, i=i, sl=sl: E.dma_start(out=xt[sl], in_=x[i * 128:(i + 1) * 128, :]),
              writes=[('xt', sl)])
        norm_stats(xt[sl], [('xt', sl)], i, xnA[i % 2], ('xn', i % 2), junk)

    id32 = mem.f32(128)
    wst = [mem.f32(8 * 512).rearrange('p (k n) -> p k n', k=8) for _ in range(3)]
    modrow = mem.f32(6 * D, parts=1)
    grow = mem.f32(3 * D, parts=1)
    gsrow = mem.f32(2 * D, parts=1)
    gbt = [mem.f32(D) for _ in range(2)]

    P.dma('sp', lambda E: E.dma_start(out=c_sb, in_=ccol), writes=['c_sb'])
    P.dma('sp', lambda E: E.dma_start(out=id32, in_=ident), writes=['id32'])
    P.dma('sp', lambda E: E.dma_start(out=modrow, in_=b_ada), writes=['modrow'])
    P.dma('sp', lambda E: E.dma_start(out=grow, in_=gvec), writes=['grow'])
    P.dma('sp', lambda E: E.dma_start(out=bgate, in_=bgate_d), writes=['bgate'])
    P.dma('sp', lambda E: E.dma_start(out=convw, in_=convw_d), writes=['convw'])
    P.add('dve', lambda E: E.tensor_copy(out=idb, in_=id32), reads=['id32'], writes=['idb'])
    P.add('dve', lambda E: E.memset(onesb, 1.0), writes=['onesb'])
    P.add('dve', lambda E: E.memset(ones32, 1.0), writes=['ones32'])
    P.add('act', lambda E: E.activation(out=c_sig, in_=c_sb, func=AF.Sigmoid),
          reads=['c_sb'], writes=['c_sig'])
    P.add('dve', lambda E: E.tensor_tensor(out=cact, in0=c_sb, in1=c_sig, op=ALU.mult),
          reads=['c_sb', 'c_sig'], writes=['cact'])

    def p0_load(cg):
        sl = cg % 3
        P.dma('sp', lambda E, cg=cg, sl=sl: E.dma_start(
            out=wst[sl], in_=w_ada[:, cg * 512:(cg + 1) * 512].rearrange('(k p) n -> p k n', p=128)),
            writes=[('wst', sl)])

    def p0_group(cg):
        sl = cg % 3
        b = 6 + cg % 2
        for k in range(8):
            P.add('pe', lambda E, k=k, sl=sl, b=b: E.matmul(
                psf(b)[0:1, :], lhsT=cact[:, k:k + 1], rhs=wst[sl][:, k, :],
                start=(k == 0), stop=(k == 7)),
                reads=[('wst', sl), 'cact'], writes=[PS(b)])
        P.add('dve', lambda E, cg=cg, b=b: E.tensor_tensor(
            out=modrow[:, cg * 512:(cg + 1) * 512], in0=psf(b)[0:1, :],
            in1=modrow[:, cg * 512:(cg + 1) * 512], op=ALU.add),
            reads=[PS(b), 'modrow'], writes=['modrow'])
        if cg + 3 < 12:
            p0_load(cg + 3)

    cgn = 0
    a_stats(0)
    for cg in range(3):
        p0_load(cg)
    for i in range(nta):
        if i + 1 < nta:
            a_stats(i + 1)
        if i % 2 == 0 and cgn < 12:
            p0_group(cgn)
            cgn += 1
        norm_te(xnA[i % 2], ('xn', i % 2),
                lambda k, i=i: hT[:, :, i * 128:(i + 1) * 128] if k is None else hT[:, k, i * 128:(i + 1) * 128],
                lambda k, i=i: [('hT', i, k)], 0, i % 2, plain=True)
    while cgn < 12:
        p0_group(cgn)
        cgn += 1
    P.add('dve', lambda E: E.scalar_tensor_tensor(
        out=gsrow[:, 0:D], in0=modrow[:, D:2 * D], scalar=1.0, in1=grow[:, 0:D],
        op0=ALU.add, op1=ALU.mult), reads=['modrow', 'grow'], writes=['gsrow'])
    P.add('dve', lambda E: E.scalar_tensor_tensor(
        out=gsrow[:, D:2 * D], in0=modrow[:, 4 * D:5 * D], scalar=1.0, in1=grow[:, D:2 * D],
        op0=ALU.add, op1=ALU.mult), reads=['modrow', 'grow', 'gsrow'], writes=['gsrow'])
    colrows = [gsrow[:, 0:D], modrow[:, 0:D], gsrow[:, D:2 * D], modrow[:, 3 * D:4 * D]]
    for vi in range(4):
        for k in range(8):
            P.add('pe', lambda E, vi=vi, k=k: E.matmul(
                psf(2)[:, vi * 8 + k: vi * 8 + k + 1], lhsT=colrows[vi][:, k * 128:(k + 1) * 128],
                rhs=ones32[0:1, 0:1], start=True, stop=True),
                reads=['gsrow', 'modrow', 'ones32'], writes=[PS(2)])
    P.add('dve', lambda E: E.tensor_copy(out=modc, in_=psf(2)[:, 0:32]), reads=[PS(2)], writes=['modc'])
    for k in range(8):
        hk = [('hT', i, k) for i in range(NT)]
        gsc = modc[:, k:k + 1]
        shc = modc[:, 8 + k:9 + k]
        if k % 2 == 0:
            P.add('act', lambda E, k=k, gsc=gsc, shc=shc: E.activation(
                out=hT[:, k, :], in_=hT[:, k, :], func=AF.Identity, bias=shc, scale=gsc),
                reads=hk + ['modc'], writes=hk)
        else:
            P.add('pool', lambda E, k=k, gsc=gsc, shc=shc: E.tensor_scalar(
                out=hT[:, k, :], in0=hT[:, k, :], scalar1=gsc, scalar2=shc, op0=ALU.mult, op1=ALU.add),
                reads=hk + ['modc'], writes=hk)
    bcrows = [modrow[:, 2 * D:3 * D], modrow[:, 5 * D:6 * D], grow[:, 2 * D:3 * D]]
    for vi in range(3):
        sl = vi % 2
        for half in range(2):
            b = 3 + half
            P.add('pe', lambda E, vi=vi, half=half, b=b: E.matmul(
                psf(b), lhsT=ones32[0:1, 0:128], rhs=bcrows[vi][:, half * 512:(half + 1) * 512],
                start=True, stop=True), reads=['modrow', 'grow', 'ones32'], writes=[PS(b)])
            P.add('dve', lambda E, sl=sl, half=half, b=b: E.tensor_copy(
                out=gbt[sl][:, half * 512:(half + 1) * 512], in_=psf(b)),
                reads=[PS(b)], writes=[('gbt', sl, half)])
        P.dma('sp', lambda E, vi=vi, sl=sl: E.dma_start(out=gbd[vi], in_=gbt[sl]),
              reads=[('gbt', sl, 0), ('gbt', sl, 1)], writes=[('gbd', vi)])
    dump('modc', modc, [128, 32], F32, ['modc'])
    hT_keys = lambda i0, i1: [('hT', i, k) for i in range(i0, i1) for k in range(8)]
    dump('hT', hT, [128, KC, S], BF16, hT_keys(0, NT))
    dump('stat', stat, [128, 256], F32, [('st', 0)])
    dump('xn', xnA[0], [128, D], BF16, [('xn', 0)])
    P.barrier()
    mem.top = mark_hT
    if stop_after == 1:
        return finish(nc, P, store_ops, dbg_out)

    oT = mem.bf16(4 * S).rearrange('p (j t) -> p j t', j=4)
    mark_oT = mem.top
    acc = mem.f32(2 * S).rearrange('p (c t) -> p c t', c=2)
    qTb = [mem.bf16(S) for _ in range(2)]
    kTb = [mem.bf16(S) for _ in range(2)]
    vtb = [mem.bf16(S).rearrange('p (b e) -> p b e', b=32) for _ in range(2)]
    vTh = mem.bf16(2048)
    wq = [mem.bf16(8 * 384).rearrange('p (k m n) -> p k m n', k=8, m=3) for _ in range(2)]
    bias4 = [mem.f32(512) for _ in range(2)]
    sbt = [mem.f32(512) for _ in range(2)]
    PT = [mem.bf16(512) for _ in range(2)]
    heads = [(j, g) for j in range(4) for g in range(3)]
    pbanks = [0, 1, 6, 7]
    bctr = {'p': 0}

    def load_w(hi):
        j, g = heads[hi]
        h = 4 * g + j
        ws = hi % 2
        for m in range(3):
            c0 = m * 1536 + h * 128
            P.dma('pool', lambda E, ws=ws, m=m, c0=c0: E.dma_start(
                out=wq[ws][:, :, m, :], in_=w_in[:, c0:c0 + 128].rearrange('(k p) n -> p k n', p=128)),
                writes=[('wq', ws, m)])

    def load_bias(hi):
        j, g = heads[hi]
        h = 4 * g + j
        ws = hi % 2
        P.dma('sp', lambda E, ws=ws, h=h: E.dma_start(out=bias4[ws], in_=abias[h]),
              writes=[('bias4', ws)])

    def proj_units(hi):
        j, g = heads[hi]
        d = DIL[g]
        nb = (S // d) // 128
        ws = hi % 2
        units = []
        for m, dstT, key in ((0, qTb[ws], ('qT', ws)), (1, kTb[ws], ('kT', ws))):
            dv = dstT.rearrange('p (r l) -> p r l', r=d)
            for tc in range(8):
                def u(m=m, dv=dv, key=key, tc=tc):
                    b = pbanks[bctr['p'] % 4]
                    bctr['p'] += 1
                    for k in range(8):
                        P.add('pe', lambda E, k=k: E.matmul(
                            psf(b), lhsT=wq[ws][:, k, m, :], rhs=hT[:, k, tc * 512:(tc + 1) * 512],
                            start=(k == 0), stop=(k == 7)),
                            reads=[('wq', ws, m)], writes=[PS(b)])
                    src = psf(b).rearrange('p (m r) -> p r m', r=d)
                    dst = dv[:, :, tc * (512 // d):(tc + 1) * (512 // d)]
                    if m == 0:
                        P.add('act', lambda E: E.activation(
                            out=dst, in_=src, func=AF.Copy, scale=float(128 ** -0.5)),
                            reads=[PS(b)], writes=[key])
                    elif g == 0:
                        P.add('dve', lambda E: E.tensor_copy(out=dst, in_=src), reads=[PS(b)], writes=[key])
                    else:
                        P.add('act', lambda E: E.activation(out=dst, in_=src, func=AF.Copy),
                              reads=[PS(b)], writes=[key])
                units.append(u)
        nbh = nb // 2
        vt4 = vtb[ws].rearrange('p (r n) e -> p r n e', r=d)
        for half in range(2):
            for c4 in range(4):
                def u(half=half, c4=c4):
                    b = pbanks[bctr['p'] % 4]
                    bctr['p'] += 1
                    tc = half * 4 + c4
                    for k in range(8):
                        P.add('pe', lambda E, k=k: E.matmul(
                            psf(b), lhsT=wq[ws][:, k, 2, :], rhs=hT[:, k, tc * 512:(tc + 1) * 512],
                            start=(k == 0), stop=(k == 7)),
                            reads=[('wq', ws, 2)], writes=[PS(b)])
                    dst = vTh[:, c4 * 512:(c4 + 1) * 512]
                    if g == 0 or c4 % 2 == 1:
                        P.add('dve', lambda E: E.tensor_copy(out=dst, in_=psf(b)),
                              reads=[PS(b)], writes=[('vTh', c4)])
                    else:
                        P.add('act', lambda E: E.activation(out=dst, in_=psf(b), func=AF.Copy),
                              reads=[PS(b)], writes=[('vTh', c4)])
                units.append(u)
            for q in range(2):
                def u(half=half, q=q):
                    b = pbanks[bctr['p'] % 4]
                    bctr['p'] += 1
                    if nbh >= 8:
                        r0, nr, n0, nn = 0, 1, nbh * half + 8 * q, 8
                    elif nbh == 4:
                        r0, nr, n0, nn = 2 * q, 2, nbh * half, 4
                    else:
                        r0, nr, n0, nn = 8 * q, 8, nbh * half, 1
                    tpv = psb(b).rearrange('p (a e) -> p a e', a=8)
                    wkeys = set()
                    idx = 0
                    for r in range(r0, r0 + nr):
                        for n in range(n0, n0 + nn):
                            st = r + d * 128 * (n - nbh * half)
                            wkeys.add(('vt', ws, (r * nb + n) // 4))
                            P.add('pe', lambda E, st=st, idx=idx: E.transpose(
                                out=tpv[:, idx, :], in_=vTh[:, st: st + d * 127 + 1: d], identity=idb),
                                reads=[('vTh', c) for c in range(4)] + ['idb'], writes=[PS(b)])
                            idx += 1
                    dstv = vt4[:, r0:r0 + nr, n0:n0 + nn, :]
                    srcv = psb(b).rearrange('p (r n e) -> p r n e', r=nr, n=nn)
                    P.add('act', lambda E: E.activation(out=dstv, in_=srcv, func=AF.Copy),
                          reads=[PS(b)], writes=sorted(wkeys))
                units.append(u)
        return units

    def att_scores(hi, pm):
        j, g = heads[hi]
        d = DIL[g]
        nb = (S // d) // 128
        ws = hi % 2
        qT, kT = qTb[ws], kTb[ws]
        qb0 = 2 * pm
        s0 = 1 if (qb0 % nb) == 0 else 0
        gp = hi * 16 + pm
        scb = 2 + (gp % 2)
        tsl = gp % 2
        rk = [('qT', ws), ('kT', ws)]
        if s0 == 0:
            P.add('pe', lambda E: E.matmul(
                psf(scb)[:, 0:128], lhsT=kT[:, (qb0 - 1) * 128:qb0 * 128], rhs=qT[:, qb0 * 128:(qb0 + 1) * 128],
                start=True, stop=True), reads=rk, writes=[PS(scb)])
        P.add('pe', lambda E: E.matmul(
            psf(scb)[:, 128:384], lhsT=kT[:, qb0 * 128:(qb0 + 1) * 128], rhs=qT[:, qb0 * 128:(qb0 + 2) * 128],
            start=True, stop=True), reads=rk, writes=[PS(scb)])
        P.add('pe', lambda E: E.matmul(
            psf(scb)[:, 384:512], lhsT=kT[:, (qb0 + 1) * 128:(qb0 + 2) * 128],
            rhs=qT[:, (qb0 + 1) * 128:(qb0 + 2) * 128],
            start=True, stop=True), reads=rk, writes=[PS(scb)])
        P.add('dve', lambda E: E.tensor_tensor(
            out=sbt[tsl][:, s0 * 128:512], in0=psf(scb)[:, s0 * 128:512],
            in1=bias4[ws][:, s0 * 128:512], op=ALU.add),
            reads=[PS(scb), ('bias4', ws)], writes=[('sbt', tsl)])
        P.add('act', lambda E: E.activation(
            out=PT[tsl][:, s0 * 128:512], in_=sbt[tsl][:, s0 * 128:512], func=AF.Exp),
            reads=[('sbt', tsl)], writes=[('PT', tsl)])

    def att_pv(hi, pm):
        j, g = heads[hi]
        d = DIL[g]
        nb = (S // d) // 128
        ws = hi % 2
        vt = vtb[ws]
        qb0 = 2 * pm
        s0 = 1 if (qb0 % nb) == 0 else 0
        gp = hi * 16 + pm
        pvb = 4 + (gp % 2)
        tsl = gp % 2
        pt = PT[tsl]
        pt4 = pt.rearrange('p (b t q) -> p b t q', b=2, t=2)
        rk = [('PT', tsl), 'onesb'] + [('vt', ws, kb // 4) for kb in (qb0 - 1, qb0, qb0 + 1) if kb >= 0]
        P.add('pe', lambda E: E.matmul(psf(pvb)[:, 0:256], lhsT=vt[:, qb0, :], rhs=pt[:, 128:384],
                                       start=True, stop=False), reads=rk, writes=[PS(pvb)])
        if s0 == 0:
            P.add('pe', lambda E: E.matmul(psf(pvb)[:, 0:128], lhsT=vt[:, qb0 - 1, :], rhs=pt[:, 0:128],
                                           start=False, stop=False), reads=rk, writes=[PS(pvb)])
        P.add('pe', lambda E: E.matmul(psf(pvb)[:, 128:256], lhsT=vt[:, qb0 + 1, :], rhs=pt[:, 384:512],
                                       start=False, stop=True), reads=rk, writes=[PS(pvb)])
        if s0 == 0:
            P.add('pe', lambda E: E.matmul(psf(pvb)[:, 256:512], lhsT=onesb, rhs=pt4[:, :, 0, :],
                                           start=True, stop=False), reads=rk, writes=[PS(pvb)])
            P.add('pe', lambda E: E.matmul(psf(pvb)[:, 256:512], lhsT=onesb, rhs=pt4[:, :, 1, :],
                                           start=False, stop=True), reads=rk, writes=[PS(pvb)])
        else:
            P.add('pe', lambda E: E.matmul(psf(pvb)[:, 256:512], lhsT=onesb, rhs=pt4[:, :, 1, :],
                                           start=True, stop=False), reads=rk, writes=[PS(pvb)])
            P.add('pe', lambda E: E.matmul(psf(pvb)[:, 384:512], lhsT=onesb, rhs=pt[:, 256:384],
                                           start=False, stop=True), reads=rk, writes=[PS(pvb)])
        r, n = divmod(qb0, nb)
        st0 = r + d * 128 * n
        accv = acc[:, :, st0: st0 + 255 * d + 1: d]
        pvv = psf(pvb).rearrange('p (c t) -> p c t', c=2)
        if g == 0:
            P.add('act', lambda E: E.activation(out=accv, in_=pvv, func=AF.Copy),
                  reads=[PS(pvb)], writes=[('acc', pm // 4)])
        else:
            aks = [('acc', q_) for q_ in range(4)]
            P.add('dve', lambda E: E.tensor_tensor(out=accv, in0=pvv, in1=accv, op=ALU.add),
                  reads=[PS(pvb)] + aks, writes=aks)

    def finalize(j):
        for q4 in range(4):
            sl_ = slice(q4 * 1024, (q4 + 1) * 1024)
            P.add('act', lambda E, sl_=sl_: E.activation(out=acc[:, 1, sl_], in_=acc[:, 1, sl_], func=AF.Ln),
                  reads=[('acc', q4)], writes=[('acc', q4)])
            P.add('act', lambda E, sl_=sl_: E.activation(out=acc[:, 1, sl_], in_=acc[:, 1, sl_], func=AF.Exp,
                                                         scale=-1.0), reads=[('acc', q4)], writes=[('acc', q4)])
            P.add('dve', lambda E, sl_=sl_, j=j: E.tensor_tensor(
                out=oT[:, j, sl_], in0=acc[:, 0, sl_], in1=acc[:, 1, sl_], op=ALU.mult),
                reads=[('acc', q4)], writes=[('oT', j, q4)])

    NHD = 12
    load_w(0)
    load_w(1)
    load_bias(0)
    for u in proj_units(0):
        u()
    precast = []
    for k in range(8):
        precast.append(lambda k=k: P.dma('pool', lambda E: E.dma_start(
            out=w1b[k * 128:(k + 1) * 128, :], in_=w1[k * 128:(k + 1) * 128, :]), writes=[('w1b', k)]))
    for k in range(8):
        precast.append(lambda k=k: P.dma('pool', lambda E: E.dma_start(
            out=w2b[k * 512:(k + 1) * 512, :].rearrange('(a p) n -> p a n', p=128),
            in_=w2[k * 512:(k + 1) * 512, :].rearrange('(a p) n -> p a n', p=128)), writes=[('w2b', k)]))
    for hi in range(NHD):
        if hi + 2 < NHD:
            load_w(hi + 2)
        if hi + 1 < NHD:
            load_bias(hi + 1)
        for _ in range(2):
            if precast:
                precast.pop(0)()
        units = proj_units(hi + 1) if hi + 1 < NHD else []
        ui = 0
        att_scores(hi, 0)
        for pm in range(16):
            if pm + 1 < 16:
                att_scores(hi, pm + 1)
            tgt = ((pm + 1) * len(units) + 15) // 16
            while ui < tgt:
                units[ui]()
                ui += 1
            att_pv(hi, pm)
        if hi % 3 == 2:
            finalize(hi // 3)
    dump('oT', oT, [128, 4, S], BF16, [('oT', j, q_) for j in range(4) for q_ in range(4)])
    P.barrier()
    mem.top = mark_oT
    if stop_after == 2:
        return finish(nc, P, store_ops, dbg_out)

    yT = mem.bf16(KC * S).rearrange('p (k t) -> p k t', k=KC)
    mark_yT = mem.top
    wc = [mem.bf16(8 * 384).rearrange('p (k m n) -> p k m n', k=8, m=3) for _ in range(3)]
    ccs = [mem.f32(512) for _ in range(2)]
    zb = [mem.f32(520) for _ in range(2)]
    ub = [mem.f32(512) for _ in range(2)]
    it = 0
    def c_load(c):
        ws = c % 3
        for m in range(3):
            c0 = 4608 + m * 1024 + c * 128
            P.dma('pool', lambda E, ws=ws, m=m, c0=c0: E.dma_start(
                out=wc[ws][:, :, m, :], in_=w_in[:, c0:c0 + 128].rearrange('(k p) n -> p k n', p=128)),
                writes=[('wc', ws, m)])

    c_load(0)
    c_load(1)
    for c in range(8):
        ws = c % 3
        if c + 2 < 8:
            c_load(c + 2)
        for tc in range(8):
            s = it % 2
            it += 1
            bb = 3 * s
            for m, b in ((1, bb), (2, bb + 1), (0, bb + 2)):
                for k in range(8):
                    P.add('pe', lambda E, ws=ws, m=m, k=k, tc=tc, b=b: E.matmul(
                        psf(b), lhsT=wc[ws][:, k, m, :], rhs=hT[:, k, tc * 512:(tc + 1) * 512],
                        start=(k == 0), stop=(k == 7)), reads=[('wc', ws, m)], writes=[PS(b)])
            P.add('act', lambda E, s=s, bb=bb: E.activation(out=ccs[s], in_=psf(bb), func=AF.Copy),
                  reads=[PS(bb)], writes=[('ccs', s)])
            P.add('dve', lambda E, s=s, bb=bb: E.tensor_tensor(
                out=zb[s][:, 2:514], in0=psf(bb + 1), in1=ccs[s], op=ALU.mult),
                reads=[PS(bb + 1), ('ccs', s)], writes=[('z', s)])
            if tc == 0:
                P.add('pool', lambda E, s=s: E.memset(zb[s][:, 0:2], 0.0), writes=[('z', s)], reads=[('z', s)])
            else:
                P.add('pool', lambda E, s=s: E.tensor_copy(out=zb[s][:, 0:2], in_=zb[1 - s][:, 512:514]),
                      reads=[('z', 1 - s), ('z', s)], writes=[('z', s)])
            w0 = convw[:, c * 3 + 0: c * 3 + 1]
            w1c = convw[:, c * 3 + 1: c * 3 + 2]
            w2c = convw[:, c * 3 + 2: c * 3 + 3]
            P.add('pool', lambda E, s=s, w0=w0: E.tensor_scalar(
                out=ub[s], in0=zb[s][:, 0:512], scalar1=w0, scalar2=0.0, op0=ALU.mult, op1=ALU.add),
                reads=[('z', s), 'convw'], writes=[('u', s)])
            P.add('dve', lambda E, s=s, w1c=w1c: E.scalar_tensor_tensor(
                out=ub[s], in0=zb[s][:, 1:513], scalar=w1c, in1=ub[s], op0=ALU.mult, op1=ALU.add),
                reads=[('z', s), ('u', s), 'convw'], writes=[('u', s)])
            P.add('dve', lambda E, s=s, w2c=w2c: E.scalar_tensor_tensor(
                out=ub[s], in0=zb[s][:, 2:514], scalar=w2c, in1=ub[s], op0=ALU.mult, op1=ALU.add),
                reads=[('z', s), ('u', s), 'convw'], writes=[('u', s)])
            P.add('dve', lambda E, s=s, bb=bb, c=c, tc=tc: E.tensor_tensor(
                out=yT[:, c, tc * 512:(tc + 1) * 512], in0=psf(bb + 2), in1=ub[s], op=ALU.mult),
                reads=[PS(bb + 2), ('u', s)], writes=[('yT', c, tc)])
    dump('yT', yT, [128, KC, S], BF16, [('yT', c, tc) for c in range(8) for tc in range(8)])
    P.barrier()
    mem.top = mark_yT
    if stop_after == 3:
        return finish(nc, P, store_ops, dbg_out)

    wd = [mem.bf16(28 * 128).rearrange('p (k n) -> p k n', k=28) for _ in range(3)]
    sat = [mem.f32(512) for _ in range(2)]
    sbt2 = [mem.f32(512) for _ in range(2)]
    tt = [mem.f32(512) for _ in range(2)]
    mout = [mem.bf16(512) for _ in range(3)]
    it = 0
    def d_load(f):
        ws = f % 3
        P.dma('pool', lambda E, ws=ws, f=f: E.dma_start(
            out=wd[ws][:, 0:4, :], in_=w_ba[:, f * 128:(f + 1) * 128].rearrange('(k p) n -> p k n', p=128)),
            writes=[('wd', ws, 0)])
        P.dma('pool', lambda E, ws=ws, f=f: E.dma_start(
            out=wd[ws][:, 4:12, :], in_=w_bc[:, f * 128:(f + 1) * 128].rearrange('(k p) n -> p k n', p=128)),
            writes=[('wd', ws, 1)])
        for m in range(2):
            c0 = 7680 + m * 1024 + f * 128
            P.dma('pool', lambda E, ws=ws, m=m, c0=c0: E.dma_start(
                out=wd[ws][:, 12 + 8 * m: 20 + 8 * m, :],
                in_=w_in[:, c0:c0 + 128].rearrange('(k p) n -> p k n', p=128)),
                writes=[('wd', ws, 2 + m)])

    d_load(0)
    d_load(1)
    for f in range(8):
        ws = f % 3
        if f + 2 < 8:
            d_load(f + 2)
        for tc in range(8):
            s = it % 2
            mo = it % 3
            it += 1
            b0 = 4 * s
            tsl = slice(tc * 512, (tc + 1) * 512)
            for k in range(4):
                P.add('pe', lambda E, ws=ws, k=k, tsl=tsl, b0=b0: E.matmul(
                    psf(b0), lhsT=wd[ws][:, k, :], rhs=oT[:, k, tsl], start=(k == 0), stop=(k == 3)),
                    reads=[('wd', ws, 0)], writes=[PS(b0)])
            for k in range(8):
                P.add('pe', lambda E, ws=ws, k=k, tsl=tsl, b0=b0: E.matmul(
                    psf(b0 + 1), lhsT=wd[ws][:, 12 + k, :], rhs=hT[:, k, tsl], start=(k == 0), stop=(k == 7)),
                    reads=[('wd', ws, 2)], writes=[PS(b0 + 1)])
            for k in range(8):
                P.add('pe', lambda E, ws=ws, k=k, tsl=tsl, b0=b0: E.matmul(
                    psf(b0 + 2), lhsT=wd[ws][:, 4 + k, :], rhs=yT[:, k, tsl], start=(k == 0), stop=(k == 7)),
                    reads=[('wd', ws, 1)], writes=[PS(b0 + 2)])
            for k in range(8):
                P.add('pe', lambda E, ws=ws, k=k, tsl=tsl, b0=b0: E.matmul(
                    psf(b0 + 3), lhsT=wd[ws][:, 20 + k, :], rhs=hT[:, k, tsl], start=(k == 0), stop=(k == 7)),
                    reads=[('wd', ws, 3)], writes=[PS(b0 + 3)])
            P.add('act', lambda E, s=s, b0=b0, f=f: E.activation(
                out=sat[s], in_=psf(b0 + 1), func=AF.Sigmoid, bias=bgate[:, f:f + 1]),
                reads=[PS(b0 + 1), 'bgate'], writes=[('sat', s)])
            P.add('act', lambda E, s=s, b0=b0, f=f: E.activation(
                out=sbt2[s], in_=psf(b0 + 3), func=AF.Sigmoid, bias=bgate[:, 8 + f:9 + f]),
                reads=[PS(b0 + 3), 'bgate'], writes=[('sbt2', s)])
            P.add('dve', lambda E, s=s, b0=b0: E.tensor_tensor(out=tt[s], in0=psf(b0), in1=sat[s], op=ALU.mult),
                  reads=[PS(b0), ('sat', s)], writes=[('tt', s)])
            P.add('dve', lambda E, s=s, b0=b0: E.tensor_tensor(out=sbt2[s], in0=psf(b0 + 2), in1=sbt2[s],
                                                               op=ALU.mult),
                  reads=[PS(b0 + 2), ('sbt2', s)], writes=[('sbt2', s)])
            P.add('pool', lambda E, s=s, mo=mo: E.tensor_tensor(out=mout[mo], in0=tt[s], in1=sbt2[s], op=ALU.add),
                  reads=[('tt', s), ('sbt2', s)], writes=[('mout', mo)])
            P.dma('sp', lambda E, mo=mo, f=f, tc=tc: E.dma_start(
                out=mrg[2 * tc:2 * tc + 2, :, f, :].rearrange('c p t -> p c t'),
                in_=mout[mo].rearrange('p (c t) -> p c t', c=2)),
                  reads=[('mout', mo)], writes=[('mrg', f, tc)])
    if 'mrg' in dbg:
        t = nc.dram_tensor('dbg_mrg', [16, 128, 8, 256], BF16, kind='ExternalOutput').ap()
        dbg_out['mrg'] = t
        store_ops.append(P.dma('sp', lambda E, t=t: E.dma_start(out=t, in_=mrg),
                               reads=[('mrg', f, tc) for f in range(8) for tc in range(8)]))
    P.barrier()
    mem.top = mark_persist
    if stop_after == 4:
        return finish(nc, P, store_ops, dbg_out)

    w1s = mem.bf16(KC * DFF).rearrange('p (k n) -> p k n', k=KC)
    w2s = mem.bf16(32 * D).rearrange('p (k n) -> p k n', k=32)
    mark_F = mem.top
    wo = mem.bf16(KC * D).rearrange('p (k n) -> p k n', k=KC)
    g1b = mem.f32(D)
    mrgc = [mem.bf16(8 * 256).rearrange('p (f t) -> p f t', f=8) for _ in range(4)]
    xe = [mem.f32(2 * D).rearrange('p (i n) -> p i n', i=2) for _ in range(4)]
    tmpe = [mem.f32(512) for _ in range(2)]
    for k in range(8):
        P.dma('pool', lambda E, k=k: E.dma_start(out=wo[:, k, :], in_=w_out[k * 128:(k + 1) * 128, :]),
              writes=[('wo', k)])
    P.dma('sp', lambda E: E.dma_start(out=g1b, in_=gbd[0]), writes=['g1b'])
    ectr = {'it': 0}

    def e_load(c):
        s = c % 4
        tsl = slice(c * 256, (c + 1) * 256)
        P.dma('sp', lambda E: E.dma_start(
            out=mrgc[s], in_=mrg[c]), writes=[('mrgc', s)])

    def e_compute(c):
        s = c % 4
        tsl = slice(c * 256, (c + 1) * 256)
        for i in range(2):
            for half in range(2):
                b = ectr['it'] % 4
                ts_ = ectr['it'] % 2
                ectr['it'] += 1
                hs = slice(half * 512, (half + 1) * 512)
                for f in range(8):
                    P.add('pe', lambda E, i=i, f=f, hs=hs, b=b: E.matmul(
                        psf(b), lhsT=mrgc[s][:, f, i * 128:(i + 1) * 128], rhs=wo[:, f, hs],
                        start=(f == 0), stop=(f == 7)),
                        reads=[('mrgc', s)] + [('wo', f)], writes=[PS(b)])
                P.add('dve', lambda E, i=i, b=b, hs=hs: E.tensor_tensor(
                    out=xe[s][:, i, hs], in0=psf(b), in1=g1b[:, hs], op=ALU.mult),
                    reads=[PS(b), 'g1b'], writes=[('xe', s, i, hs.start)])
            P.dma('sp', lambda E, i=i: E.dma_start(
                out=x1s[c * 256 + i * 128:c * 256 + (i + 1) * 128, :], in_=xe[s][:, i, :]),
                reads=[('xe', s, i, h0) for h0 in (0, 512)], writes=[('x1s', c, i)])

    for c in range(4):
        e_load(c)
    wpieces = []
    for k in range(8):
        wpieces.append(lambda k=k: P.dma('sp', lambda E: E.dma_start(
            out=w1s[:, k, :], in_=w1b[k * 128:(k + 1) * 128, :]), writes=[('w1s', k)]))
    for k4 in range(8):
        wpieces.append(lambda k4=k4: P.dma('sp', lambda E: E.dma_start(
            out=w2s[:, k4 * 4:(k4 + 1) * 4, :],
            in_=w2b[k4 * 512:(k4 + 1) * 512, :].rearrange('(a p) n -> p a n', p=128)),
            writes=[('w2s', k4 * 4 + a) for a in range(4)]))
    for _ in range(2):
        wpieces.pop(0)()
    for c in range(16):
        e_compute(c)
        if c + 4 < 16:
            e_load(c + 4)
        if wpieces:
            wpieces.pop(0)()
    while wpieces:
        wpieces.pop(0)()
    if 'x1' in dbg:
        t = nc.dram_tensor('dbg_x1', [S, D], F32, kind='ExternalOutput').ap()
        dbg_out['x1'] = t
        store_ops.append(P.dma('sp', lambda E, t=t: E.dma_start(out=t, in_=x1s), reads=[('x1s', c, i) for c in range(16) for i in range(2)]))
    P.barrier()
    mem.top = mark_F
    if stop_after == 5:
        return finish(nc, P, store_ops, dbg_out)

    fT = mem.bf16(32 * 256).rearrange('p (j t) -> p j t', j=32)
    g2b = mem.f32(D)
    gfb = mem.f32(D)
    x1t = [mem.f32(2 * D).rearrange('p (i n) -> p i n', i=2) for _ in range(2)]
    dtmp = mem.f32(2 * D).rearrange('p (i n) -> p i n', i=2)
    h2T = [mem.bf16(KC * 256).rearrange('p (k t) -> p k t', k=KC) for _ in range(2)]
    xnF = mem.bf16(D)
    junkF = mem.bf16(D)
    rl = [mem.f32(512) for _ in range(2)]
    tmpf = [mem.f32(512) for _ in range(2)]
    P.dma('sp', lambda E: E.dma_start(out=g2b, in_=gbd[1]), writes=['g2b'])
    P.dma('sp', lambda E: E.dma_start(out=gfb, in_=gbd[2]), writes=['gfb'])
    xnF2 = [xnF, mem.bf16(D)]
    ctr = {'mi': 0, 'mo': 0, 'sti': 0}

    def f_load_stats(ch):
        s = ch % 2
        rows = slice(ch * 256, (ch + 1) * 256)
        P.dma('sp', lambda E, s=s, rows=rows: E.dma_start(
            out=x1t[s], in_=x[rows, :].rearrange('(i p) n -> p i n', p=128)),
            writes=[('x1t', s, 0), ('x1t', s, 1)])
        P.dma('sp', lambda E, rows=rows: E.dma_start(
            out=dtmp, in_=x1s[rows, :].rearrange('(i p) n -> p i n', p=128)), writes=['dtmp'])
        for i in range(2):
            P.add('pool', lambda E, s=s, i=i: E.tensor_tensor(
                out=x1t[s][:, i, :], in0=dtmp[:, i, :], in1=x1t[s][:, i, :], op=ALU.add),
                reads=['dtmp', ('x1t', s, i)], writes=[('x1t', s, i)])
        for i in range(2):
            norm_stats(x1t[s][:, i, :], [('x1t', s, i)], ctr['sti'], xnF2[i], ('xnF', i), junkF)
            ctr['sti'] += 1

    def f_te(ch):
        s = ch % 2
        for i in range(2):
            norm_te(xnF2[i], ('xnF', i), lambda k, s=s, i=i: h2T[s][:, k, i * 128:(i + 1) * 128],
                    lambda k, s=s, i=i: [('h2T', s, i, k)], 16, i % 2)

    def f_mlp_in(ch):
        s = ch % 2
        h2keys = [('h2T', s, i, k) for i in range(2) for k in range(8)]
        for jp in range(16):
            b = 2 + (ctr['mi'] % 3)
            rs_ = ctr['mi'] % 2
            ctr['mi'] += 1
            for jj in range(2):
                jf = jp * 2 + jj
                for k in range(8):
                    P.add('pe', lambda E, s=s, jf=jf, jj=jj, k=k, b=b: E.matmul(
                        psf(b)[:, jj * 256:(jj + 1) * 256], lhsT=w1s[:, k, jf * 128:(jf + 1) * 128],
                        rhs=h2T[s][:, k, :], start=(k == 0), stop=(k == 7)),
                        reads=h2keys + [('w1s', k)], writes=[PS(b)])
            P.add('act', lambda E, b=b, rs_=rs_: E.activation(out=rl[rs_], in_=psf(b), func=AF.Relu),
                  reads=[PS(b)], writes=[('rl', rs_)])
            P.add('dve', lambda E, b=b, rs_=rs_, jp=jp: E.tensor_tensor(
                out=fT[:, jp * 2:jp * 2 + 2, :], in0=psf(b).rearrange('p (j t) -> p j t', j=2),
                in1=rl[rs_].rearrange('p (j t) -> p j t', j=2), op=ALU.mult),
                reads=[PS(b), ('rl', rs_)], writes=[('fT', jp)])

    def f_mlp_out(ch, i):
        s = ch % 2
        for half in range(2):
            b = 5 + (ctr['mo'] % 3)
            ts_ = ctr['mo'] % 2
            ctr['mo'] += 1
            hs = slice(half * 512, (half + 1) * 512)
            for jf in range(32):
                P.add('pe', lambda E, jf=jf, i=i, hs=hs, b=b: E.matmul(
                    psf(b), lhsT=fT[:, jf, i * 128:(i + 1) * 128], rhs=w2s[:, jf, hs],
                    start=(jf == 0), stop=(jf == 31)),
                    reads=[('fT', jf // 2), ('w2s', jf)], writes=[PS(b)])
            P.add('dve', lambda E, b=b, ts_=ts_, hs=hs: E.tensor_tensor(
                out=tmpf[ts_], in0=psf(b), in1=g2b[:, hs], op=ALU.mult),
                reads=[PS(b), 'g2b'], writes=[('tmpf', ts_)])
            P.add('pool', lambda E, s=s, i=i, ts_=ts_, hs=hs: E.tensor_tensor(
                out=x1t[s][:, i, hs], in0=tmpf[ts_], in1=x1t[s][:, i, hs], op=ALU.add),
                reads=[('tmpf', ts_), ('x1t', s, i)], writes=[('x1t', s, i)])
        src = x1t[s][:, i, :]
        ci = ctr['sti'] % 64
        ctr['sti'] += 1
        ssc = stat[:, ci:ci + 1]
        msc = stat[:, 64 + ci:65 + ci]
        rsc = stat[:, 128 + ci:129 + ci]
        rdc = stat[:, 192 + ci:193 + ci]
        ks = ('st', ci)
        P.add('act', lambda E, src=src, ssc=ssc: E.activation(out=junkF, in_=src, func=AF.Square, accum_out=ssc),
              reads=[('x1t', s, i)], writes=['junk', ks])
        P.add('dve', lambda E, ssc=ssc, msc=msc: E.tensor_scalar(
            out=msc, in0=ssc, scalar1=1.0 / D, scalar2=EPS, op0=ALU.mult, op1=ALU.add), reads=[ks], writes=[ks])
        P.add('act', lambda E, rsc=rsc, msc=msc: E.activation(out=rsc, in_=msc, func=AF.Sqrt),
              reads=[ks], writes=[ks])
        P.add('dve', lambda E, rsc=rsc, rdc=rdc: E.reciprocal(out=rdc, in_=rsc), reads=[ks], writes=[ks])
        P.add('dve', lambda E, src=src, rdc=rdc: E.scalar_tensor_tensor(
            out=src, in0=src, scalar=rdc, in1=gfb, op0=ALU.mult, op1=ALU.mult),
            reads=[('x1t', s, i), ks, 'gfb'], writes=[('x1t', s, i)])

    def f_store(ch, i):
        s = ch % 2
        r0 = ch * 256 + i * 128
        store_ops.append(P.dma('sp', lambda E, s=s, r0=r0, i=i: E.dma_start(
            out=out[r0:r0 + 128, :], in_=x1t[s][:, i, :]),
            reads=[('x1t', s, i)], writes=[('out', ch, i)]))

    f_load_stats(0)
    f_te(0)
    for ch in range(16):
        f_mlp_in(ch)
        if ch + 1 < 16:
            f_load_stats(ch + 1)
        f_mlp_out(ch, 0)
        f_store(ch, 0)
        if ch + 1 < 16:
            f_te(ch + 1)
        f_mlp_out(ch, 1)
        f_store(ch, 1)
    return finish(nc, P, store_ops, dbg_out)


def finish(nc, P, store_ops, dbg_out):
    P.add('sp', None, extra=store_ops)
    P.emit(nc)
    return nc, dbg_out


def _alibi_bias_table():
    slopes = 2.0 ** (-8.0 * np.arange(1, N_HEADS + 1, dtype=np.float64) / N_HEADS)
    kj = np.arange(128)[:, None]
    qi = np.arange(128)[None, :]
    tab = np.zeros((N_HEADS, 128, 4, 128), np.float32)
    for h in range(N_HEADS):
        d = DIL[h // 4]
        c = slopes[h] * d
        dprev = 128 + qi - kj
        prev = np.where(dprev <= 128, -c * dprev, NEG)
        dcur = qi - kj
        cur = np.where(dcur >= 0, -c * dcur, NEG)
        for b in range(2):
            tab[h, :, 2 * b + 0, :] = prev
            tab[h, :, 2 * b + 1, :] = cur
    return tab.reshape(N_HEADS, 128, 512)


def make_in_maps(inputs, cores):
    f = lambda a: np.ascontiguousarray(np.asarray(a, dtype=np.float32))
    x = f(inputs['x'])
    c = f(inputs['c'])
    shared = {
        'w_ada': f(inputs['w_ada'][0]),
        'b_ada': f(inputs['b_ada'][0]).reshape(1, -1),
        'gvec': np.concatenate([f(inputs['g_norm_mix'][0]), f(inputs['g_norm_mlp'][0]),
                                f(inputs['g_norm_final'])]).reshape(1, -1),
        'w_in': f(inputs['w_in'][0]),
        'bgate': f(f(inputs['b_gate'][0]).reshape(16, 128).T),
        'convw': f(f(inputs['conv_w'][0]).T.reshape(8, 128, 3).transpose(1, 0, 2).reshape(128, 24)),
        'w_ba': f(inputs['w_branch_attn'][0]),
        'w_bc': f(inputs['w_branch_conv'][0]),
        'w_out': f(inputs['w_out'][0]),
        'w1': f(inputs['w_mlp_in'][0]),
        'w2': f(inputs['w_mlp_out'][0]),
        'abias': _alibi_bias_table(),
        'ident': np.eye(128, dtype=np.float32),
    }
    maps = []
    for b in cores:
        m = dict(shared)
        m['x'] = f(x[b])
        m['ccol'] = f(c[b].reshape(8, 128).T)
        maps.append(m)
    return maps


_CACHE = {}


def kernel(**inputs):
    if 'nc' not in _CACHE:
        _CACHE['nc'] = build_program()[0]
    nc = _CACHE['nc']
    cores = list(range(8))
    in_maps = make_in_maps(inputs, cores)
    res = run_bass_kernel_spmd(nc, in_maps, core_ids=cores)
    return np.stack([np.asarray(r['out'], dtype=np.float32) for r in res.results], axis=0)
```
